# Optimizing a Trainium2 kernel written in Bass

```python
import jax, jax.numpy as jnp
from jax import lax
import numpy as np

D_MODEL = 4096
BATCH = 8
SEQ = 2048
DEPTH = 2

HEAD_DIM = 128
N_HEADS_TOTAL = D_MODEL // HEAD_DIM
NSA_HEADS = N_HEADS_TOTAL // 2
NSA_KV_HEADS = NSA_HEADS // 4
NSA_GROUP = NSA_HEADS // NSA_KV_HEADS
SB_HEADS = N_HEADS_TOTAL - NSA_HEADS
CMP_BLOCK = 32
CMP_STRIDE = 16
SEL_BLOCK = 64
SEL_TOP_N = 16
WINDOW = 512
Q_BLOCK = 128
SEL_Q_BLOCK = 32
ROPE_THETA = 10000.0
POOL_GROUPS = 4
POOL_WINDOWS = (2, 4, 8, 16)
POOL_GROUP_DIM = D_MODEL // POOL_GROUPS
D_FF = 11008
CONV_WIDTH = 3
LN_EPS = 1e-5
NEG_INF = -1e30
FORCE_SCORE = 1e9
DEEPNORM_ALPHA = (2 * DEPTH) ** 0.25
DEEPNORM_BETA = (8 * DEPTH) ** -0.25
N_ATTN_LAYERS = (DEPTH + 1) // 2
N_POOL_LAYERS = DEPTH // 2

Q_NSA_DIM = NSA_HEADS * HEAD_DIM
KV_NSA_DIM = NSA_KV_HEADS * HEAD_DIM
GATE_DIM = 3 * NSA_HEADS
SB_DIM = SB_HEADS * HEAD_DIM
IN_SEGMENTS = (Q_NSA_DIM, KV_NSA_DIM, KV_NSA_DIM, KV_NSA_DIM, KV_NSA_DIM, KV_NSA_DIM, KV_NSA_DIM,
               GATE_DIM, SB_DIM, SB_DIM, SB_DIM)
IN_VALUE_SEGMENT = (False, False, True, False, True, False, True, False, False, False, True)
IN_DIM = sum(IN_SEGMENTS)

kernel_name = "nsa_stickbreak_pool_convffn_deepnorm"


def layer_norm(x, g, b):
    xf = x.astype(jnp.float32)
    mu = xf.mean(-1, keepdims=True)
    var = jnp.square(xf - mu).mean(-1, keepdims=True)
    return ((xf - mu) * lax.rsqrt(var + LN_EPS) * g + b).astype(x.dtype)


def rope_cos_sin(pos):
    inv_freq = 1.0 / (ROPE_THETA ** (jnp.arange(0, HEAD_DIM, 2, dtype=jnp.float32) / HEAD_DIM))
    ang = pos.astype(jnp.float32)[:, None] * inv_freq[None, :]
    return jnp.cos(ang), jnp.sin(ang)


def apply_rope(x, cos, sin):
    xf = x.astype(jnp.float32)
    x1, x2 = xf[..., :HEAD_DIM // 2], xf[..., HEAD_DIM // 2:]
    c, s = cos[None, :, None, :], sin[None, :, None, :]
    return jnp.concatenate([x1 * c - x2 * s, x1 * s + x2 * c], axis=-1).astype(x.dtype)


def nsa_attention(q, k_cmp, v_cmp, k_slc, v_slc, k_win, v_win, gate_logits,
                  w_cmp_k, w_cmp_v, pe_cmp_k, pe_cmp_v):
    B, T = q.shape[:2]
    scale = HEAD_DIM ** -0.5
    qg = q.reshape(B, T, NSA_KV_HEADS, NSA_GROUP, HEAD_DIM)
    t_pos = jnp.arange(T)

    n_cmp = (T - CMP_BLOCK) // CMP_STRIDE + 1
    blk_idx = np.arange(n_cmp)[:, None] * CMP_STRIDE + np.arange(CMP_BLOCK)[None, :]
    kc_blocks = k_cmp[:, blk_idx] + pe_cmp_k[None, None, :, None, :]
    vc_blocks = v_cmp[:, blk_idx] + pe_cmp_v[None, None, :, None, :]
    kc = jnp.einsum('bnlhd,lde->bnhe', kc_blocks, w_cmp_k)
    vc = jnp.einsum('bnlhd,lde->bnhe', vc_blocks, w_cmp_v)
    cmp_end = np.arange(n_cmp) * CMP_STRIDE + CMP_BLOCK - 1
    c_cos, c_sin = rope_cos_sin(jnp.asarray(cmp_end))
    kc = apply_rope(kc, c_cos, c_sin)
    valid_cmp = jnp.asarray(cmp_end)[None, :] <= t_pos[:, None]
    s_cmp = jnp.einsum('btkgd,bnkd->bkgtn', qg, kc).astype(jnp.float32) * scale
    s_cmp = jnp.where(valid_cmp, s_cmp, NEG_INF)
    p_cmp = jax.nn.softmax(s_cmp, axis=-1) * valid_cmp.any(-1, keepdims=True)
    o_cmp = jnp.einsum('bkgtn,bnkd->btkgd', p_cmp.astype(vc.dtype), vc)

    n_blk = T // SEL_BLOCK
    c_start = np.arange(n_cmp)[:, None] * CMP_STRIDE
    b_start = np.arange(n_blk)[None, :] * SEL_BLOCK
    overlap = np.clip(np.minimum(c_start + CMP_BLOCK, b_start + SEL_BLOCK) - np.maximum(c_start, b_start), 0, None)
    overlap = jnp.asarray(overlap / CMP_BLOCK, jnp.float32)
    imp = jnp.einsum('bkgtn,nj->bktj', p_cmp, overlap)
    blk = jnp.arange(n_blk)[None, :]
    cur = (t_pos // SEL_BLOCK)[:, None]
    forced = (blk == 0) | (blk == cur) | (blk == cur - 1)
    valid_blk = blk * SEL_BLOCK <= t_pos[:, None]
    imp = jnp.where(forced, FORCE_SCORE, jnp.where(valid_blk, imp, NEG_INF))
    n_top = min(SEL_TOP_N, n_blk)
    _, sel_idx = lax.top_k(imp, n_top)

    ks_blocks = k_slc.reshape(B, n_blk, SEL_BLOCK, NSA_KV_HEADS, HEAD_DIM).transpose(0, 3, 1, 2, 4)
    vs_blocks = v_slc.reshape(B, n_blk, SEL_BLOCK, NSA_KV_HEADS, HEAD_DIM).transpose(0, 3, 1, 2, 4)
    n_sq = T // SEL_Q_BLOCK
    q_sel = qg.reshape(B, n_sq, SEL_Q_BLOCK, NSA_KV_HEADS, NSA_GROUP, HEAD_DIM).transpose(1, 0, 2, 3, 4, 5)
    idx_sel = sel_idx.reshape(B, NSA_KV_HEADS, n_sq, SEL_Q_BLOCK, n_top).transpose(2, 0, 1, 3, 4)
    gather = jax.vmap(jax.vmap(lambda blocks, ids: blocks[ids]))

    def sel_chunk(args):
        qc, ic, start = args
        kg = gather(ks_blocks, ic)
        vg = gather(vs_blocks, ic)
        s = jnp.einsum('bqkgd,bkqnld->bkgqnl', qc, kg).astype(jnp.float32) * scale
        kpos = ic[..., None] * SEL_BLOCK + jnp.arange(SEL_BLOCK)
        tq = start + jnp.arange(SEL_Q_BLOCK)
        mask = kpos <= tq[None, None, :, None, None]
        s = jnp.where(mask[:, :, None], s, NEG_INF)
        p = jax.nn.softmax(s.reshape(s.shape[:4] + (-1,)), axis=-1).reshape(s.shape)
        return jnp.einsum('bkgqnl,bkqnld->bqkgd', p.astype(vg.dtype), vg)

    o_slc = lax.map(sel_chunk, (q_sel, idx_sel, jnp.arange(n_sq) * SEL_Q_BLOCK))
    o_slc = o_slc.transpose(1, 0, 2, 3, 4, 5).reshape(B, T, NSA_KV_HEADS, NSA_GROUP, HEAD_DIM)

    kw_pad = jnp.pad(k_win, ((0, 0), (WINDOW, 0), (0, 0), (0, 0)))
    vw_pad = jnp.pad(v_win, ((0, 0), (WINDOW, 0), (0, 0), (0, 0)))
    n_qb = T // Q_BLOCK
    q_win = qg.reshape(B, n_qb, Q_BLOCK, NSA_KV_HEADS, NSA_GROUP, HEAD_DIM).transpose(1, 0, 2, 3, 4, 5)

    def win_chunk(args):
        qc, start = args
        kb = lax.dynamic_slice_in_dim(kw_pad, start, WINDOW + Q_BLOCK, axis=1)
        vb = lax.dynamic_slice_in_dim(vw_pad, start, WINDOW + Q_BLOCK, axis=1)
        s = jnp.einsum('bqkgd,bskd->bkgqs', qc, kb).astype(jnp.float32) * scale
        tq = start + jnp.arange(Q_BLOCK)
        kpos = start - WINDOW + jnp.arange(WINDOW + Q_BLOCK)
        diff = tq[:, None] - kpos[None, :]
        mask = (diff >= 0) & (diff < WINDOW) & (kpos >= 0)[None, :]
        p = jax.nn.softmax(jnp.where(mask, s, NEG_INF), axis=-1)
        return jnp.einsum('bkgqs,bskd->bqkgd', p.astype(vb.dtype), vb)

    o_win = lax.map(win_chunk, (q_win, jnp.arange(n_qb) * Q_BLOCK))
    o_win = o_win.transpose(1, 0, 2, 3, 4, 5).reshape(B, T, NSA_KV_HEADS, NSA_GROUP, HEAD_DIM)

    g = jax.nn.sigmoid(gate_logits.astype(jnp.float32)).astype(q.dtype)
    g = g.reshape(B, T, 3, NSA_KV_HEADS, NSA_GROUP, 1)
    o = g[:, :, 0] * o_cmp + g[:, :, 1] * o_slc + g[:, :, 2] * o_win
    return o.reshape(B, T, NSA_HEADS * HEAD_DIM)


def stick_breaking_attention(q, k, v):
    B, T, H, _ = q.shape
    scale = HEAD_DIM ** -0.5
    n_qb = T // Q_BLOCK
    q_blocks = q.reshape(B, n_qb, Q_BLOCK, H, HEAD_DIM).transpose(1, 0, 2, 3, 4)
    kpos = jnp.arange(T)

    def chunk(args):
        qc, start = args
        z = jnp.einsum('bqhd,bshd->bhqs', qc, k).astype(jnp.float32) * scale
        tq = start + jnp.arange(Q_BLOCK)
        mask = kpos[None, :] < tq[:, None]
        neg_log_1m_beta = jnp.where(mask, jax.nn.softplus(z), 0.0)
        later = lax.cumsum(neg_log_1m_beta, axis=3, reverse=True) - neg_log_1m_beta
        a = jnp.where(mask, jnp.exp(jax.nn.log_sigmoid(z) - later), 0.0)
        return jnp.einsum('bhqs,bshd->bqhd', a.astype(v.dtype), v)

    o = lax.map(chunk, (q_blocks, jnp.arange(n_qb) * Q_BLOCK))
    return o.transpose(1, 0, 2, 3, 4).reshape(B, T, H * HEAD_DIM)


def attention_mixer(x, w_in, w_out, w_cmp_k, w_cmp_v, pe_cmp_k, pe_cmp_v):
    B, T, _ = x.shape
    h = jnp.einsum('btd,de->bte', x, w_in)
    cuts = [int(c) for c in np.cumsum(IN_SEGMENTS)[:-1]]
    q_a, kc, vc, ks, vs, kw, vw, gates, q_b, k_b, v_b = jnp.split(h, cuts, axis=-1)
    heads = lambda t, n: t.reshape(B, T, n, HEAD_DIM)
    cos, sin = rope_cos_sin(jnp.arange(T))
    q_a = apply_rope(heads(q_a, NSA_HEADS), cos, sin)
    ks = apply_rope(heads(ks, NSA_KV_HEADS), cos, sin)
    kw = apply_rope(heads(kw, NSA_KV_HEADS), cos, sin)
    o_a = nsa_attention(q_a, heads(kc, NSA_KV_HEADS), heads(vc, NSA_KV_HEADS), ks, heads(vs, NSA_KV_HEADS),
                        kw, heads(vw, NSA_KV_HEADS), gates, w_cmp_k, w_cmp_v, pe_cmp_k, pe_cmp_v)
    o_b = stick_breaking_attention(heads(q_b, SB_HEADS), heads(k_b, SB_HEADS), heads(v_b, SB_HEADS))
    return jnp.einsum('bte,ed->btd', jnp.concatenate([o_a, o_b], axis=-1), w_out)


def pool_mixer(x, w_pool, pool_scale):
    B, T, D = x.shape
    cs = jnp.cumsum(x.astype(jnp.float32), axis=1)
    t = jnp.arange(T)
    diffs = []
    for gi, w in enumerate(POOL_WINDOWS):
        sl = slice(gi * POOL_GROUP_DIM, (gi + 1) * POOL_GROUP_DIM)
        csg = cs[..., sl]
        prev = jnp.pad(csg, ((0, 0), (w, 0), (0, 0)))[:, :T]
        count = jnp.minimum(t + 1, w).astype(jnp.float32)[None, :, None]
        diffs.append((csg - prev) / count - x[..., sl].astype(jnp.float32))
    d = jnp.stack(diffs, axis=2).astype(x.dtype)
    y = jnp.einsum('btgc,gce->btge', d, w_pool).reshape(B, T, D)
    return y * pool_scale


def conv_ffn(x, w_up, conv_w, conv_b, w_down):
    T = x.shape[1]
    h = jnp.einsum('btd,df->btf', x, w_up)
    hp = jnp.pad(h, ((0, 0), (CONV_WIDTH - 1, 0), (0, 0)))
    hc = conv_b
    for k in range(CONV_WIDTH):
        hc = hc + hp[:, k:k + T] * conv_w[k]
    gate, val = jnp.split(hc, 2, axis=-1)
    return jnp.einsum('btf,fd->btd', jax.nn.silu(gate) * val, w_down)


def setup_inputs(seed: int = 0) -> dict:
    key = jax.random.key(seed)
    ks = jax.random.split(key, 18)
    nrm = lambda k, shape, s: jax.random.normal(k, shape, jnp.float32) * s
    col_scale = np.concatenate([np.full(n, DEEPNORM_BETA if v else 1.0, np.float32)
                                for n, v in zip(IN_SEGMENTS, IN_VALUE_SEGMENT)])
    return {
        "x": nrm(ks[0], (BATCH, SEQ, D_MODEL), 1.0),
        "attn_w_in": nrm(ks[1], (N_ATTN_LAYERS, D_MODEL, IN_DIM), D_MODEL ** -0.5) * jnp.asarray(col_scale),
        "attn_w_out": nrm(ks[2], (N_ATTN_LAYERS, D_MODEL, D_MODEL), D_MODEL ** -0.5 * DEEPNORM_BETA),
        "cmp_w_k": nrm(ks[3], (N_ATTN_LAYERS, CMP_BLOCK, HEAD_DIM, HEAD_DIM), (CMP_BLOCK * HEAD_DIM) ** -0.5),
        "cmp_w_v": nrm(ks[4], (N_ATTN_LAYERS, CMP_BLOCK, HEAD_DIM, HEAD_DIM), (CMP_BLOCK * HEAD_DIM) ** -0.5),
        "cmp_pe_k": nrm(ks[5], (N_ATTN_LAYERS, CMP_BLOCK, HEAD_DIM), 0.1),
        "cmp_pe_v": nrm(ks[6], (N_ATTN_LAYERS, CMP_BLOCK, HEAD_DIM), 0.1),
        "pool_w": nrm(ks[7], (N_POOL_LAYERS, POOL_GROUPS, POOL_GROUP_DIM, POOL_GROUP_DIM),
                       POOL_GROUP_DIM ** -0.5 * DEEPNORM_BETA),
        "pool_scale": 1.0 + nrm(ks[8], (N_POOL_LAYERS, D_MODEL), 0.1),
        "ffn_w_up": nrm(ks[9], (DEPTH, D_MODEL, 2 * D_FF), D_MODEL ** -0.5),
        "ffn_conv_w": nrm(ks[10], (DEPTH, CONV_WIDTH, 2 * D_FF), CONV_WIDTH ** -0.5),
        "ffn_conv_b": nrm(ks[11], (DEPTH, 2 * D_FF), 0.02),
        "ffn_w_down": nrm(ks[12], (DEPTH, D_FF, D_MODEL), D_FF ** -0.5 * DEEPNORM_BETA),
        "ln_mix_g": 1.0 + nrm(ks[13], (DEPTH, D_MODEL), 0.05),
        "ln_mix_b": nrm(ks[14], (DEPTH, D_MODEL), 0.02),
        "ln_ffn_g": 1.0 + nrm(ks[15], (DEPTH, D_MODEL), 0.05),
        "ln_ffn_b": nrm(ks[16], (DEPTH, D_MODEL), 0.02),
    }


def reference(x, attn_w_in, attn_w_out, cmp_w_k, cmp_w_v, cmp_pe_k, cmp_pe_v, pool_w, pool_scale,
              ffn_w_up, ffn_conv_w, ffn_conv_b, ffn_w_down, ln_mix_g, ln_mix_b, ln_ffn_g, ln_ffn_b):
    for layer in range(DEPTH):
        i = layer // 2
        if layer % 2 == 0:
            m = attention_mixer(x, attn_w_in[i], attn_w_out[i], cmp_w_k[i], cmp_w_v[i], cmp_pe_k[i], cmp_pe_v[i])
        else:
            m = pool_mixer(x, pool_w[i], pool_scale[i])
        x = layer_norm(DEEPNORM_ALPHA * x + m, ln_mix_g[layer], ln_mix_b[layer])
        f = conv_ffn(x, ffn_w_up[layer], ffn_conv_w[layer], ffn_conv_b[layer], ffn_w_down[layer])
        x = layer_norm(DEEPNORM_ALPHA * x + f, ln_ffn_g[layer], ln_ffn_b[layer])
    return x
```

```python
import numpy as np
import ml_dtypes
from contextlib import ExitStack
import concourse.bass as bass
import concourse.mybir as mybir
from concourse.bass_utils import run_bass_kernel_spmd

F32 = mybir.dt.float32
BF16 = mybir.dt.bfloat16
AF = mybir.ActivationFunctionType
ALU = mybir.AluOpType
AX = mybir.AxisListType

ENGS = ('sp', 'act', 'pe', 'dve', 'pool')
ALPHA = float((2 * 2) ** 0.25)
LN_EPS = 1e-5


class Prog:
    def __init__(self, nc, es, ndma=10):
        self.nc = nc
        self.ndma = ndma
        self.sems = {}
        self.val = {}
        for e in ENGS:
            self.sems[e] = es.enter_context(nc.semaphore("c_" + e))
            self.val[e] = 0
        for q in ('sp', 'act', 'pool'):
            for i in range(ndma):
                k = (q, i)
                self.sems[k] = es.enter_context(nc.semaphore("d_%s%d" % (q, i)))
                self.val[k] = 0
        self.dnext = {'sp': 0, 'act': 0, 'pool': 0}
        self.seen = {e: {} for e in ENGS}
        self.lastw = {}
        self.readers = {}
        self.prog = {e: [] for e in ENGS}
        self.pending = {e: [] for e in ENGS}
        self.pend_keys = {}
        self.bank = 0
        self.ninstr = 0

    def _need(self, e, reads, writes):
        need = {}
        for r in reads:
            pk = self.pend_keys.get(r)
            if pk is not None and pk[1] and pk[0] != e:
                raise RuntimeError("dependency on pending write %r" % (r,))
            ev = self.lastw.get(r)
            if ev is not None and need.get(ev[0], 0) < ev[1]:
                need[ev[0]] = ev[1]
        for w in writes:
            pk = self.pend_keys.get(w)
            if pk is not None and pk[0] != e:
                raise RuntimeError("dependency on pending access %r" % (w,))
            ev = self.lastw.get(w)
            if ev is not None and need.get(ev[0], 0) < ev[1]:
                need[ev[0]] = ev[1]
            rd = self.readers.get(w)
            if rd:
                for k, v in rd.items():
                    if need.get(k, 0) < v:
                        need[k] = v
        return need

    def _sync(self, e, reads, writes):
        need = self._need(e, reads, writes)
        seen = self.seen[e]
        for k, v in need.items():
            if k == e and e == 'pe':
                continue
            if seen.get(k, 0) >= v:
                continue
            self.prog[e].append(('w', k, v))
            seen[k] = v

    def _record(self, ev, reads, writes):
        for w in writes:
            self.lastw[w] = ev
            self.readers[w] = {}
        for r in reads:
            d = self.readers.get(r)
            if d is None:
                d = self.readers[r] = {}
            if d.get(ev[0], 0) < ev[1]:
                d[ev[0]] = ev[1]

    def ins(self, e, build, reads=(), writes=(), inc=True):
        self.ninstr += 1
        pr = [r for r in reads if isinstance(r, tuple) and r[0] == 'ps']
        if pr:
            reads = [r for r in reads if not (isinstance(r, tuple) and r[0] == 'ps')]
            writes = list(writes) + [r for r in pr if r not in writes]
        self._sync(e, reads, writes)
        if inc:
            self.val[e] += 1
            ev = (e, self.val[e])
            self.prog[e].append(('i', build, e, 1))
            self._record(ev, reads, writes)
            if self.pending[e]:
                for (r, w) in self.pending[e]:
                    self._record(ev, r, w)
                    for k in r:
                        self.pend_keys.pop(k, None)
                    for k in w:
                        self.pend_keys.pop(k, None)
                self.pending[e] = []
        else:
            self.prog[e].append(('i', build, None, 0))
            self.pending[e].append((tuple(reads), tuple(writes)))
            for k in reads:
                if k not in self.pend_keys:
                    self.pend_keys[k] = (e, False)
            for k in writes:
                self.pend_keys[k] = (e, True)

    def dma(self, q, out, in_, reads=(), writes=()):
        self.ninstr += 1
        i = self.dnext[q]
        self.dnext[q] = (i + 1) % self.ndma
        k = (q, i)
        if self.val[k] > 0 and self.seen[q].get(k, 0) < self.val[k]:
            self.prog[q].append(('w', k, self.val[k]))
            self.seen[q][k] = self.val[k]
        self._sync(q, reads, writes)
        self.val[k] += 16
        ev = (k, self.val[k])
        self.prog[q].append(('i', lambda eng, o=out, i_=in_: eng.dma_start(out=o, in_=i_), k, 16))
        self._record(ev, reads, writes)

    def barrier(self):
        for e in ENGS:
            assert not self.pending[e], "pending instrs at barrier on " + e
        for e in ENGS:
            seen = self.seen[e]
            for k, v in self.val.items():
                if v > 0 and seen.get(k, 0) < v:
                    self.prog[e].append(('w', k, v))
                    seen[k] = v
        self.lastw.clear()
        self.readers.clear()

    def emit(self):
        sems = self.sems
        with self.nc.Block() as block:
            for e, meth in (('sp', block.sync), ('act', block.scalar), ('pe', block.tensor),
                            ('dve', block.vector), ('pool', block.gpsimd)):
                items = self.prog[e]

                def body(eng, items=items):
                    for it in items:
                        if it[0] == 'w':
                            eng.wait_ge(sems[it[1]], it[2])
                        else:
                            r = it[1](eng)
                            if it[2] is not None:
                                r.then_inc(sems[it[2]], it[3])
                meth(body)
        self.prog = {e: [] for e in ENGS}

    def next_bank(self):
        b = self.bank
        self.bank = (b + 1) % 8
        return b

    def mm(self, out, lhsT, rhs, start, stop, reads, writes, inc):
        self.ins('pe', lambda eng: eng.matmul(out, lhsT, rhs, start=start, stop=stop), reads, writes, inc)

    def transpose(self, out, in_, ident, reads, writes, inc):
        self.ins('pe', lambda eng: eng.transpose(out, in_, ident), reads, writes, inc)

    def act(self, out, in_, func, reads, writes, bias=0.0, scale=1.0, e='act'):
        self.ins('act', lambda eng: eng.activation(out, in_, func, bias=bias, scale=scale), reads, writes)

    def copy(self, e, out, in_, reads, writes):
        if e == 'act':
            self.ins('act', lambda eng: eng.activation(out, in_, AF.Copy), reads, writes)
        else:
            self.ins(e, lambda eng: eng.tensor_copy(out, in_), reads, writes)

    def tt(self, e, out, in0, in1, op, reads, writes):
        self.ins(e, lambda eng: eng.tensor_tensor(out, in0, in1, op), reads, writes)

    def ts(self, e, out, in0, s1, s2, op0, op1, reads, writes):
        if s2 is None:
            self.ins(e, lambda eng: eng.tensor_scalar(out, in0, s1, None, op0), reads, writes)
        else:
            self.ins(e, lambda eng: eng.tensor_scalar(out, in0, s1, s2, op0, op1), reads, writes)

    def stt(self, e, out, in0, scalar, in1, op0, op1, reads, writes):
        self.ins(e, lambda eng: eng.scalar_tensor_tensor(out, in0, scalar, in1, op0, op1), reads, writes)


_uid = [0]


def sb(nc, es, name, shape, dt):
    _uid[0] += 1
    return es.enter_context(nc.sbuf_tensor("%s_%d" % (name, _uid[0]), list(shape), dt))


def load_xT(P, nc, es, PS, x_in, XT, ident, T, D, tag="x"):
    NC = D // 128
    xs = [sb(nc, es, "%s_xs%d" % (tag, i), [128, D], BF16) for i in range(3)]
    cp = 0
    for i in range(T // 128):
        xb = xs[i % 3]
        kx = (tag + 'xs', i % 3)
        P.dma('pool', xb[:], x_in[i * 128:(i + 1) * 128, :], reads=[], writes=[kx])
        for g in range(NC // 8):
            b = P.next_bank()
            pb = PS[b][:].bitcast(BF16)
            for m in range(8):
                c = g * 8 + m
                P.transpose(pb[:, m * 128:(m + 1) * 128], xb[:, c * 128:(c + 1) * 128], ident,
                            reads=[kx], writes=[('ps', b)], inc=(m == 7))
            dst = XT[:, g * 8:(g + 1) * 8, i * 128:(i + 1) * 128]
            src = pb.rearrange("p (a b) -> p a b", a=8)
            P.copy('act' if cp % 2 == 0 else 'dve', dst, src, reads=[('ps', b)], writes=[('XT', i)])
            cp += 1


def layer_norm_pass(P, nc, es, Y, x_out, lng, lnb, T, D, tag="ln", XT=None, ident=None, PS=None):
    gb = sb(nc, es, tag + "_g", [128, D], F32)
    bb = sb(nc, es, tag + "_b", [128, D], F32)
    P.dma('sp', gb[:], lng.partition_broadcast(128), writes=[(tag, 'g')])
    P.dma('sp', bb[:], lnb.partition_broadcast(128), writes=[(tag, 'b')])
    NB = 4 if XT is None else 2
    if XT is not None:
        ybf = sb(nc, es, tag + "_ybf", [128, D], BF16)
    yb = [sb(nc, es, "%s_y%d" % (tag, i), [128, D], F32) for i in range(NB)]
    st = [sb(nc, es, "%s_st%d" % (tag, i), [128, D // 512, 6], F32) for i in range(NB)]
    mv = [sb(nc, es, "%s_mv%d" % (tag, i), [128, 2], F32) for i in range(NB)]
    rs = [sb(nc, es, "%s_rs%d" % (tag, i), [128, 2], F32) for i in range(NB)]
    for i in range(T // 128):
        s = i % NB
        y = yb[s]
        ky = (tag + 'y', s)
        P.dma('sp', y[:], Y[i * 128:(i + 1) * 128, :], writes=[ky])
        kst = (tag + 'st', s)
        for c in range(D // 512):
            P.ins('dve', lambda eng, o=st[s][:, c, :], a=y[:, c * 512:(c + 1) * 512]: eng.bn_stats(o, a),
                  reads=[ky], writes=[(tag + 'st', s, c)])
        P.ins('dve', lambda eng, o=mv[s][:], a=st[s][:]: eng.bn_aggr(o, a),
              reads=[(tag + 'st', s, c) for c in range(D // 512)], writes=[(tag + 'mv', s)])
        P.ts('dve', rs[s][:, 0:1], mv[s][:, 1:2], LN_EPS, None, ALU.add, None,
             reads=[(tag + 'mv', s)], writes=[(tag + 'rs', s, 0)])
        P.act(rs[s][:, 0:1], rs[s][:, 0:1], AF.Sqrt, reads=[(tag + 'rs', s, 0)], writes=[(tag + 'rs', s, 0)])
        P.ins('dve', lambda eng, o=rs[s][:, 0:1]: eng.reciprocal(o, o),
              reads=[(tag + 'rs', s, 0)], writes=[(tag + 'rs', s, 0)])
        P.stt('dve', rs[s][:, 1:2], mv[s][:, 0:1], -1.0, rs[s][:, 0:1], ALU.mult, ALU.mult,
              reads=[(tag + 'mv', s), (tag + 'rs', s, 0)], writes=[(tag + 'rs', s, 1)])
        P.act(y[:], y[:], AF.Identity, reads=[ky, (tag + 'rs', s, 0), (tag + 'rs', s, 1)], writes=[ky],
              bias=rs[s][:, 1:2], scale=rs[s][:, 0:1])
        P.tt('dve', y[:], y[:], gb[:], ALU.mult, reads=[ky, (tag, 'g')], writes=[ky, (ky, 'lo'), (ky, 'hi')])
        XS = D // 4
        P.tt('pool', y[:, XS:], y[:, XS:], bb[:, XS:], ALU.add, reads=[ky, (tag, 'b')], writes=[(ky, 'hi')])
        P.tt('dve', y[:, 0:XS], y[:, 0:XS], bb[:, 0:XS], ALU.add, reads=[ky, (tag, 'b')], writes=[(ky, 'lo')])
        P.dma('pool', x_out[i * 128:(i + 1) * 128, :], y[:], reads=[ky, (ky, 'lo'), (ky, 'hi')], writes=[])
        if XT is not None:
            P.copy('act', ybf[:], y[:], reads=[ky, (ky, 'lo'), (ky, 'hi')], writes=[tag + 'ybf'])
            for g in range(D // 1024):
                b = P.next_bank()
                pb = PS[b][:].bitcast(BF16)
                for m in range(8):
                    c = g * 8 + m
                    P.transpose(pb[:, m * 128:(m + 1) * 128], ybf[:, c * 128:(c + 1) * 128], ident,
                                reads=[tag + 'ybf'], writes=[('ps', b)], inc=(m == 7))
                P.copy('act', XT[:, g * 8:(g + 1) * 8, i * 128:(i + 1) * 128],
                       pb.rearrange("p (a b) -> p a b", a=8), reads=[('ps', b)], writes=[('XT', i)])


def layer_norm_tiles(P, nc, es, PS, Y, STd, x_out, lng, lnb, T, D, tag, XT=None, ident=None):
    NI = T // 128
    NB = D // 512
    CW = 1024
    NCB = D // CW
    gb = sb(nc, es, tag + "_g", [128, D], F32)
    bb = sb(nc, es, tag + "_b", [128, D], F32)
    P.dma('sp', gb[:], lng.partition_broadcast(128), writes=[(tag, 'g')])
    P.dma('sp', bb[:], lnb.partition_broadcast(128), writes=[(tag, 'b')])
    ST = sb(nc, es, tag + "_ST", [128, NI, NB, 6], F32)
    P.dma('sp', ST[:].rearrange("p a b c -> p (a b c)"), STd, writes=[tag + 'ST'])
    MV = sb(nc, es, tag + "_MV", [128, NI, 2], F32)
    RS = sb(nc, es, tag + "_RS", [128, NI, 2], F32)
    for i in range(NI):
        P.ins('dve', lambda eng, i=i: eng.bn_aggr(MV[:, i, :], ST[:, i, :, :]), reads=[tag + 'ST'], writes=[(tag + 'MV', i)])
    mvk = [(tag + 'MV', i) for i in range(NI)]
    P.ts('dve', RS[:, :, 0:1], MV[:, :, 1:2], LN_EPS, None, ALU.add, None, reads=mvk, writes=[tag + 'RS0'])
    P.act(RS[:, :, 0:1], RS[:, :, 0:1], AF.Sqrt, reads=[tag + 'RS0'], writes=[tag + 'RS0'])
    P.ins('dve', lambda eng: eng.reciprocal(RS[:, :, 0:1], RS[:, :, 0:1]), reads=[tag + 'RS0'], writes=[tag + 'RS0'])
    P.stt('dve', RS[:, :, 1:2], MV[:, :, 0:1], -1.0, RS[:, :, 0:1], ALU.mult, ALU.mult,
          reads=mvk + [tag + 'RS0'], writes=[tag + 'RS1'])
    NYB = 6
    yt = Rot(tag + 'yt', [sb(nc, es, "%s_yt%d" % (tag, i), [128, CW], F32) for i in range(NYB)])
    if XT is not None:
        ybf = Rot(tag + 'ybf', [sb(nc, es, "%s_ybf%d" % (tag, i), [128, CW], BF16) for i in range(3)])
    n = 0
    for i in range(NI):
        for cb in range(NCB):
            csl = slice(cb * CW, (cb + 1) * CW)
            y, ky = yt.next()
            P.dma('sp', y[:], Y[i * 128:(i + 1) * 128, csl], writes=[ky])
            P.act(y[:], y[:], AF.Identity, reads=[ky, tag + 'RS0', tag + 'RS1'], writes=[ky],
                  bias=RS[:, i, 1:2], scale=RS[:, i, 0:1])
            P.tt('dve', y[:], y[:], gb[:, csl], ALU.mult, reads=[ky, (tag, 'g')], writes=[ky])
            P.tt('pool' if n % 2 == 0 else 'dve', y[:], y[:], bb[:, csl], ALU.add, reads=[ky, (tag, 'b')], writes=[ky])
            n += 1
            P.dma('pool', x_out[i * 128:(i + 1) * 128, csl], y[:], reads=[ky], writes=[])
            if XT is not None:
                yb, kyb = ybf.next()
                P.copy('act', yb[:], y[:], reads=[ky], writes=[kyb])
                b = P.next_bank()
                pb = PS[b][:].bitcast(BF16)
                for m in range(8):
                    P.transpose(pb[:, m * 128:(m + 1) * 128], yb[:, m * 128:(m + 1) * 128], ident,
                                reads=[kyb], writes=[('ps', b)], inc=(m == 7))
                P.copy('act' if n % 2 == 0 else 'dve', XT[:, cb * 8:(cb + 1) * 8, i * 128:(i + 1) * 128],
                       pb.rearrange("p (a b) -> p a b", a=8), reads=[('ps', b)], writes=[('XT', i, cb)])


def final_ln_phase(P, nc, PS, ident_d, Y, x_out, lng, lnb, T, D, tag, want_xt, STd=None):
    if not want_xt:
        with ExitStack() as es:
            if STd is not None:
                layer_norm_tiles(P, nc, es, PS, Y, STd, x_out, lng, lnb, T, D, tag)
            else:
                layer_norm_pass(P, nc, es, Y, x_out, lng, lnb, T, D, tag=tag)
            P.barrier()
            P.emit()
        return None
    esX = ExitStack()
    XT = sb(nc, esX, tag + "_XTn", [128, D // 128, T], BF16)
    with ExitStack() as es:
        ident = sb(nc, es, tag + "_ident", [128, 128], BF16)
        P.dma('pool', ident[:], ident_d, writes=[tag + 'ident'])
        if STd is not None:
            layer_norm_tiles(P, nc, es, PS, Y, STd, x_out, lng, lnb, T, D, tag, XT=XT, ident=ident[:])
        else:
            layer_norm_pass(P, nc, es, Y, x_out, lng, lnb, T, D, tag=tag, XT=XT, ident=ident[:], PS=PS)
        P.barrier()
        P.emit()
    return (esX, XT)


def stage_ffn(P, nc, PS, ident_d, x_in, x_out, w_up, convp, w_down, lng, lnb, G, Y, T, D, FC, xt_in=None, STd=None):
    NC = D // 128
    NT = T // 512
    F = FC * 128
    wup_v = w_up.rearrange("(c p) n -> p c n", p=128)
    wdn_v = w_down.rearrange("(c p) n -> p c n", p=128)
    with ExitStack() as es:
        if xt_in is None:
            ident = sb(nc, es, "f_ident", [128, 128], BF16)
            P.dma('pool', ident[:], ident_d, writes=['ident'])
            XT = sb(nc, es, "f_XT", [128, NC, T], BF16)
            with ExitStack() as es0:
                load_xT(P, nc, es0, PS, x_in, XT, ident[:], T, D, tag="f")
                P.barrier()
                P.emit()
        else:
            XT = xt_in[1]
        cw = sb(nc, es, "f_cw", [128, 4, 2 * FC], F32)
        P.dma('sp', cw[:], convp.rearrange("p (k c) -> p k c", k=4), writes=['cw'])
        NWB = 2
        wg = [sb(nc, es, "f_wg%d" % i, [128, NC, 128], BF16) for i in range(NWB)]
        wv = [sb(nc, es, "f_wv%d" % i, [128, NC, 128], BF16) for i in range(NWB)]
        hb = [[sb(nc, es, "f_h%d_%d" % (i, s), [128, 514], F32) for s in range(2)] for i in range(2)]
        ab = [[sb(nc, es, "f_a%d_%d" % (i, s), [128, 512], F32) for s in range(2)] for i in range(2)]
        sg = [sb(nc, es, "f_sg%d" % s, [128, 512], F32) for s in range(2)]
        go = [sb(nc, es, "f_go%d" % i, [128, T], BF16) for i in range(2)]
        xt_keys = [('XT', i) for i in range(T // 128)]
        it = 0
        def load_w(j):
            wb = j % NWB
            P.dma('pool', wg[wb][:], wup_v[:, :, j * 128:(j + 1) * 128], writes=[('wg', wb)])
            P.dma('pool', wv[wb][:], wup_v[:, :, F + j * 128:F + (j + 1) * 128], writes=[('wv', wb)])
        load_w(0)
        for j in range(FC):
            wb = j % NWB
            if j + 1 < FC:
                load_w(j + 1)
            gob = go[j % 2]
            kgo = ('go', j % 2)
            for tt in range(NT):
                s = it % 2
                it += 1
                banks = []
                for (w, kw) in ((wg[wb], ('wg', wb)), (wv[wb], ('wv', wb))):
                    b = P.next_bank()
                    banks.append(b)
                    for c in range(NC):
                        P.mm(PS[b][:], w[:, c, :], XT[:, c, tt * 512:(tt + 1) * 512], c == 0, c == NC - 1,
                             reads=[kw], writes=[('ps', b)], inc=(c == NC - 1))
                for gv in range(2):
                    h = hb[gv][s]
                    a = ab[gv][s]
                    kh = ('h', gv, s)
                    ka = ('a', gv, s)
                    col = gv * FC + j
                    if tt == 0:
                        P.ins('dve', lambda eng, o=h[:, 0:2]: eng.memset(o, 0.0), writes=[(kh, 'halo')])
                    else:
                        hp = hb[gv][1 - s]
                        P.copy('act', h[:, 0:2], hp[:, 512:514], reads=[('h', gv, 1 - s)], writes=[(kh, 'halo')])
                    P.copy('act', h[:, 2:514], PS[banks[gv]][:], reads=[('ps', banks[gv])], writes=[kh])
                    P.ts('dve', a[:], h[:, 2:514], cw[:, 2, col:col + 1], cw[:, 3, col:col + 1], ALU.mult, ALU.add,
                         reads=[kh, 'cw'], writes=[ka])
                    P.stt('dve', a[:], h[:, 1:513], cw[:, 1, col:col + 1], a[:], ALU.mult, ALU.add,
                          reads=[kh, (kh, 'halo'), ka, 'cw'], writes=[ka])
                    P.stt('dve', a[:], h[:, 0:512], cw[:, 0, col:col + 1], a[:], ALU.mult, ALU.add,
                          reads=[kh, (kh, 'halo'), ka, 'cw'], writes=[ka])
                P.act(sg[s][:], ab[0][s][:], AF.Silu, reads=[('a', 0, s)], writes=[('sg', s)])
                P.tt('dve', gob[:, tt * 512:(tt + 1) * 512], sg[s][:], ab[1][s][:], ALU.mult,
                     reads=[('sg', s), ('a', 1, s)], writes=[kgo])
            P.dma('sp', G[:, :, j, :].rearrange("i p t -> p i t"), gob[:].rearrange("p (i t) -> p i t", t=128),
                  reads=[kgo], writes=['G'])
        P.barrier()
        P.emit()
    if xt_in is not None:
        xt_in[0].close()
    DB = 512
    with ExitStack() as es:
        wd = sb(nc, es, "f_wd", [128, FC, DB], BF16)
        gt = [sb(nc, es, "f_gt%d" % i, [128, FC, 128], BF16) for i in range(2)]
        xr = [sb(nc, es, "f_xr%d" % i, [128, DB], F32) for i in range(2)]
        yo = [sb(nc, es, "f_yo%d" % i, [128, DB], F32) for i in range(2)]
        if STd is not None:
            ST = sb(nc, es, "f_ST", [128, T // 128, D // 512, 6], F32)
        NSPL = 8
        bounds = [(FC * q) // NSPL for q in range(NSPL + 1)]
        ends = set(bounds[1:])
        it = 0
        for db in range(D // DB):
            for q in range(NSPL):
                lo, hi = bounds[q], bounds[q + 1]
                if hi <= lo:
                    continue
                P.dma('pool', wd[:, lo:hi, :], wdn_v[:, lo:hi, db * DB:(db + 1) * DB], writes=[('wd', q)])
            for i in range(T // 128):
                s = it % 2
                it += 1
                P.dma('sp', gt[s][:], G[i], reads=['G'], writes=[('gt', s)])
                P.dma('sp', xr[s][:], x_in[i * 128:(i + 1) * 128, db * DB:(db + 1) * DB], writes=[('xr', s)])
                b = P.next_bank()
                for k in range(FC):
                    q = 0
                    while k >= bounds[q + 1]:
                        q += 1
                    P.mm(PS[b][:], gt[s][:, k, :], wd[:, k, :], k == 0, k == FC - 1,
                         reads=[('gt', s), ('wd', q)], writes=[('ps', b)], inc=((k + 1) in ends))
                P.stt('dve', yo[s][:], xr[s][:], ALPHA, PS[b][:], ALU.mult, ALU.add,
                      reads=[('xr', s), ('ps', b)], writes=[('yo', s)])
                if STd is not None:
                    P.ins('dve', lambda eng, o=ST[:, i, db, :], a=yo[s][:]: eng.bn_stats(o, a),
                          reads=[('yo', s)], writes=[('ST', i, db)])
                P.dma('act', Y[i * 128:(i + 1) * 128, db * DB:(db + 1) * DB], yo[s][:], reads=[('yo', s)], writes=['Y'])
        if STd is not None:
            P.dma('sp', STd, ST[:].rearrange("p a b c -> p (a b c)"),
                  reads=[('ST', i, db) for i in range(T // 128) for db in range(D // DB)], writes=['STd'])
        P.barrier()
        P.emit()
    with ExitStack() as es:
        if STd is not None:
            layer_norm_tiles(P, nc, es, PS, Y, STd, x_out, lng, lnb, T, D, "fl")
        else:
            layer_norm_pass(P, nc, es, Y, x_out, lng, lnb, T, D, tag="fl")
        P.barrier()
        P.emit()


POOL_WINDOWS = (2, 4, 8, 16)


def pool_consts():
    out = np.zeros((128, 12, 128), np.float32)
    s = np.arange(128)[:, None]
    t = np.arange(128)[None, :]
    for g, w in enumerate(POOL_WINDOWS):
        inwin = ((t - s) >= 0) & ((t - s) < w)
        out[:, 3 * g + 0, :] = inwin / float(w) - (s == t)
        pw = ((t - (s - 128)) >= 0) & ((t - (s - 128)) < w)
        out[:, 3 * g + 1, :] = pw / float(w)
        cnt = np.minimum(t + 1, w).astype(np.float32)
        out[:, 3 * g + 2, :] = inwin / cnt - (s == t)
    return out


def stage_pool(P, nc, PS, x_in, x_out, pool_w, pool_scale, poolc_d, lng, lnb, Y, T, D, ident_d=None, want_xt=False,
               STd=None):
    NC = D // 128
    NG = 4
    CG = NC // NG
    GD = D // NG
    NI = T // 128
    with ExitStack() as es:
        DT = sb(nc, es, "p_DT", [128, NC, T], BF16)
        with ExitStack() as es0:
            pc = sb(nc, es0, "p_pc", [128, 12, 128], F32)
            P.dma('sp', pc[:], poolc_d, writes=['pc'])
            xs = [sb(nc, es0, "p_xs%d" % i, [128, D], F32) for i in range(3)]
            cp = 0
            for i in range(NI):
                xb = xs[i % 3]
                P.dma('sp', xb[:], x_in[i * 128:(i + 1) * 128, :], writes=[('pxs', i % 3)])
                for c4 in range(NC // 4):
                    b = P.next_bank()
                    for m in range(4):
                        c = c4 * 4 + m
                        g = c // CG
                        o = PS[b][:, m * 128:(m + 1) * 128]
                        bm = pc[:, 3 * g + (2 if i == 0 else 0), :]
                        last = (i == 0)
                        P.mm(o, xb[:, c * 128:(c + 1) * 128], bm, True, last,
                             reads=[('pxs', i % 3), 'pc'], writes=[('ps', b)], inc=(last and m == 3))
                        if i > 0:
                            xp = xs[(i - 1) % 3]
                            P.mm(o, xp[:, c * 128:(c + 1) * 128], pc[:, 3 * g + 1, :], False, True,
                                 reads=[('pxs', (i - 1) % 3), 'pc'], writes=[('ps', b)], inc=(m == 3))
                    dst = DT[:, c4 * 4:(c4 + 1) * 4, i * 128:(i + 1) * 128]
                    src = PS[b][:].rearrange("p (a b) -> p a b", a=4)
                    P.copy('act' if cp % 2 == 0 else 'dve', dst, src, reads=[('ps', b)], writes=[('DT', i)])
                    cp += 1
            P.barrier()
            P.emit()
        psc = sb(nc, es, "p_psc", [128, D], F32)
        P.dma('sp', psc[:], pool_scale.partition_broadcast(128), writes=['psc'])
        wp = [sb(nc, es, "p_wp%d" % i, [128, CG, GD], BF16) for i in range(2)]
        xr = [sb(nc, es, "p_xr%d" % i, [128, 512], F32) for i in range(2)]
        t1 = [sb(nc, es, "p_t1%d" % i, [128, 512], F32) for i in range(2)]
        yo = [sb(nc, es, "p_yo%d" % i, [128, 512], F32) for i in range(2)]
        if STd is not None:
            ST = sb(nc, es, "p_ST", [128, NI, D // 512, 6], F32)
        it = 0
        for g in range(NG):
            w = wp[g % 2]
            P.dma('pool', w[:], pool_w[g].rearrange("(c p) e -> p c e", p=128), writes=[('wp', g % 2)])
            for i in range(NI):
                for eb in range(GD // 512):
                    s = it % 2
                    it += 1
                    c0 = g * GD + eb * 512
                    P.dma('act', xr[s][:], x_in[i * 128:(i + 1) * 128, c0:c0 + 512], writes=[('xr', s)])
                    b = P.next_bank()
                    for c in range(CG):
                        P.mm(PS[b][:], DT[:, g * CG + c, i * 128:(i + 1) * 128], w[:, c, eb * 512:(eb + 1) * 512],
                             c == 0, c == CG - 1, reads=[('wp', g % 2)], writes=[('ps', b)], inc=(c == CG - 1))
                    P.tt('dve', t1[s][:], PS[b][:], psc[:, c0:c0 + 512], ALU.mult,
                         reads=[('ps', b), 'psc'], writes=[('t1', s)])
                    P.stt('dve', yo[s][:], xr[s][:], ALPHA, t1[s][:], ALU.mult, ALU.add,
                          reads=[('xr', s), ('t1', s)], writes=[('yo', s)])
                    if STd is not None:
                        P.ins('dve', lambda eng, o=ST[:, i, c0 // 512, :], a=yo[s][:]: eng.bn_stats(o, a),
                              reads=[('yo', s)], writes=[('ST', i, c0 // 512)])
                    P.dma('sp', Y[i * 128:(i + 1) * 128, c0:c0 + 512], yo[s][:], reads=[('yo', s)], writes=['Y'])
        if STd is not None:
            P.dma('sp', STd, ST[:].rearrange("p a b c -> p (a b c)"),
                  reads=[('ST', i, cb) for i in range(NI) for cb in range(D // 512)], writes=['STd'])
        P.barrier()
        P.emit()
    return final_ln_phase(P, nc, PS, ident_d, Y, x_out, lng, lnb, T, D, "pl", want_xt, STd=STd)


HD = 128
SEG = dict(qa=0, kc=2048, vc=2560, ks=3072, vs=3584, kw=4096, vw=4608, gt=5120, qb=5168, kb=7216, vb=9264)
IN_DIM = 11312
SCALE = float(128 ** -0.5)
BIG = 30000.0
TSEQ = 2048
NCMP = 127


def attn_consts():
    T = TSEQ
    c = {}
    c['ident'] = np.eye(128, dtype=np.float32)
    rot = np.zeros((128, 128), np.float32)
    for dp in range(64):
        rot[dp + 64, dp] = -1.0
    for dp in range(64, 128):
        rot[dp - 64, dp] = 1.0
    c['rotm'] = rot
    inv_freq = (1.0 / (10000.0 ** (np.arange(0, 128, 2, dtype=np.float32) / 128.0))).astype(np.float32)

    def cs(pos):
        ang = pos.astype(np.float32)[:, None] * inv_freq[None, :]
        co = np.cos(ang).astype(np.float32).T
        si = np.sin(ang).astype(np.float32).T
        return np.concatenate([co, co], 0), np.concatenate([si, si], 0)
    c['cos'], c['sin'] = cs(np.arange(T))
    cmp_end = np.arange(NCMP) * 16 + 31
    c['cosc'], c['sinc'] = cs(cmp_end)
    t = np.arange(T)
    nv = np.where(cmp_end[:, None] <= t[None, :], 0.0, -BIG).astype(np.float32)
    c['negvalid'] = np.concatenate([nv, np.zeros((1, T), np.float32)], 0)
    c_start = np.arange(NCMP)[:, None] * 16
    b_start = np.arange(32)[None, :] * 64
    ov = np.clip(np.minimum(c_start + 32, b_start + 64) - np.maximum(c_start, b_start), 0, None) / 32.0
    vx = np.zeros((128, 33), np.float32)
    vx[:NCMP, 0] = 1.0
    vx[:NCMP, 1:] = ov
    c['vcx'] = vx
    blk = np.arange(32)[None, :]
    cur = (t // 64)[:, None]
    forced = (blk == 0) | (blk == cur) | (blk == cur - 1)
    valid = blk * 64 <= t[:, None]
    vnf = (valid & ~forced).astype(np.float32)
    addc = np.where(forced, 1e9, np.where(valid, 0.0, -1e30)).astype(np.float32)
    c['vnf'] = vnf.reshape(16, 128, 32).transpose(1, 0, 2).copy()
    c['addc'] = addc.reshape(16, 128, 32).transpose(1, 0, 2).copy()
    em = np.zeros((128, 16, 128), np.float32)
    for a in range(16):
        for s in range(128):
            em[(a * 128 + s) // 64, a, s] = 1.0
    c['expm'] = em
    sl = np.arange(128)[:, None]
    tl = np.arange(512)[None, :]
    negw = np.zeros((128, 8, 512), np.float32)
    for r in range(-4, 4):
        d = tl - 128 * r - sl
        negw[:, r + 4, :] = np.where((d >= 0) & (d < 512), 0.0, -BIG)
    c['negw'] = negw
    negsb = np.zeros((128, 4, 512), np.float32)
    for r in range(4):
        d = tl - 128 * r - sl
        negsb[:, r, :] = np.where(d > 0, 0.0, -BIG)
    c['negsb'] = negsb
    j = np.arange(128)[:, None]
    s = np.arange(128)[None, :]
    c['tri'] = (j > s).astype(np.float32)
    c['onesmtri'] = (j <= s).astype(np.float32)
    return c


class ConstBlob:
    def __init__(self, consts):
        self.off = {}
        cols = []
        o = 0
        for k, v in consts.items():
            v2 = v.reshape(128, -1)
            self.off[k] = (o, v2.shape[1], v.shape)
            cols.append(v2)
            o += v2.shape[1]
        self.arr = np.ascontiguousarray(np.concatenate(cols, 1).astype(np.float32))
        self.ncols = o

    def load(self, P, nc, es, blob_d, name, dt, q='pool'):
        o, n, shape = self.off[name]
        t = sb(nc, es, "c_" + name, list(shape), dt)
        src = blob_d[:, o:o + n]
        dst = t[:]
        if len(shape) == 3:
            src = src.rearrange("p (a b) -> p a b", a=shape[1])
        P.dma(q, dst, src, writes=[('const', name)])
        return t


class Rot:
    def __init__(self, name, bufs):
        self.name = name
        self.bufs = bufs
        self.i = 0

    def next(self):
        i = self.i
        self.i = (i + 1) % len(self.bufs)
        return self.bufs[i], (self.name, i)


def stage_attn(P, nc, PS, CB, blob_d, x_in, x_out, w_in, w_out, cmp_wk, cmp_wv, pekT, pevT, lng, lnb, S, T, D, stop=99,
               want_xt=False, STd=None):
    NC = D // 128
    NI = T // 128
    NT = T // 512
    win_v = w_in.rearrange("(c p) n -> p c n", p=128)
    QA, KCR, VCR, KS, KW, QB, KB = S['QA'], S['KCR'], S['VCR'], S['KS'], S['KW'], S['QB'], S['KB']
    VS, VW, VB, GTMd, Y = S['VS'], S['VW'], S['VB'], S['GTM'], S['Y']
    with ExitStack() as es:
        XT = sb(nc, es, "a_XT", [128, NC, T], BF16)
        ident = CB.load(P, nc, es, blob_d, 'ident', BF16)
        rotm = CB.load(P, nc, es, blob_d, 'rotm', BF16)
        cosT = CB.load(P, nc, es, blob_d, 'cos', F32, q='sp')
        sinT = CB.load(P, nc, es, blob_d, 'sin', F32, q='sp')
        with ExitStack() as es0:
            load_xT(P, nc, es0, PS, x_in, XT, ident[:], T, D, tag="a")
            P.barrier()
            P.emit()
        wb = Rot('wb', [sb(nc, es, "a_wb%d" % i, [128, NC, 256], BF16) for i in range(2)])
        ost = Rot('ost', [sb(nc, es, "a_ost%d" % i, [128, T], BF16) for i in range(2)])
        qraw = Rot('qraw', [sb(nc, es, "a_qr%d" % i, [128, 512], BF16) for i in range(2)])
        t1r = Rot('t1', [sb(nc, es, "a_t1%d" % i, [128, 512], F32) for i in range(1)])
        t2r = Rot('t2', [sb(nc, es, "a_t2%d" % i, [128, 512], F32) for i in range(1)])
        vst = Rot('vst', [sb(nc, es, "a_vst%d" % i, [128, NI, 256], BF16) for i in range(1)])
        gtm = sb(nc, es, "a_gtm", [128, NI, 48], F32)
        cpi = [0]
        fm = [('qa', 16, True, QA), ('kc', 4, False, KCR), ('vc', 4, False, VCR), ('ks', 4, True, KS),
              ('kw', 4, True, KW), ('qb', 16, False, QB), ('kb', 16, False, KB)]
        for (sname, nch, rope, dest) in fm:
            for ch in range(nch):
                if ch % 2 == 0:
                    col = SEG[sname] + ch * 128
                    w, kw_ = wb.next()
                    P.dma('pool', w[:], win_v[:, :, col:col + 256], writes=[kw_])
                wsl = slice((ch % 2) * 128, (ch % 2) * 128 + 128)
                o, ko = ost.next()
                for tt in range(NT):
                    b = P.next_bank()
                    tsl = slice(tt * 512, (tt + 1) * 512)
                    for c in range(NC):
                        P.mm(PS[b][:], w[:, c, wsl], XT[:, c, tsl], c == 0, c == NC - 1,
                             reads=[kw_], writes=[('ps', b)], inc=(c == NC - 1))
                    if rope:
                        qr, kq = qraw.next()
                        P.copy('act', qr[:], PS[b][:], reads=[('ps', b)], writes=[kq])
                        b2 = P.next_bank()
                        P.mm(PS[b2][:], rotm[:], qr[:], True, True, reads=[kq, ('const', 'rotm')],
                             writes=[('ps', b2)], inc=True)
                        t1, k1 = t1r.next()
                        t2, k2 = t2r.next()
                        P.tt('dve', t1[:], PS[b][:], cosT[:, tsl], ALU.mult, reads=[('ps', b), ('const', 'cos')],
                             writes=[k1])
                        P.tt('dve', t2[:], PS[b2][:], sinT[:, tsl], ALU.mult, reads=[('ps', b2), ('const', 'sin')],
                             writes=[k2])
                        P.tt('pool', o[:, tsl], t1[:], t2[:], ALU.add, reads=[k1, k2], writes=[ko])
                    else:
                        P.copy('act' if cpi[0] % 2 == 0 else 'dve', o[:, tsl], PS[b][:], reads=[('ps', b)], writes=[ko])
                        cpi[0] += 1
                P.dma('sp', dest[ch], o[:], reads=[ko], writes=[('dram', sname)])
        w, kw_ = wb.next()
        P.dma('pool', w[:, :, 0:48], win_v[:, :, SEG['gt']:SEG['gt'] + 48], writes=[kw_])
        for i in range(NI):
            b = P.next_bank()
            for c in range(NC):
                P.mm(PS[b][:, 0:48], XT[:, c, i * 128:(i + 1) * 128], w[:, c, 0:48], c == 0, c == NC - 1,
                     reads=[kw_], writes=[('ps', b)], inc=(c == NC - 1))
            P.act(gtm[:, i, :], PS[b][:, 0:48], AF.Sigmoid, reads=[('ps', b)], writes=['gtm'])
        P.dma('sp', GTMd, gtm[:].rearrange("p a b -> p (a b)"), reads=['gtm'], writes=[('dram', 'gtm')])
        for (sname, ncol, dest) in (('vs', 512, VS), ('vw', 512, VW), ('vb', 2048, VB)):
            for sl_ in range(ncol // 256):
                col = SEG[sname] + sl_ * 256
                w, kw_ = wb.next()
                P.dma('pool', w[:], win_v[:, :, col:col + 256], writes=[kw_])
                vt, kv = vst.next()
                for i2 in range(NI // 2):
                    b = P.next_bank()
                    for m in range(2):
                        i = i2 * 2 + m
                        for c in range(NC):
                            P.mm(PS[b][:, m * 256:(m + 1) * 256], XT[:, c, i * 128:(i + 1) * 128], w[:, c, :],
                                 c == 0, c == NC - 1, reads=[kw_], writes=[('ps', b)],
                                 inc=(c == NC - 1 and m == 1))
                    P.copy('act' if cpi[0] % 2 == 0 else 'dve', vt[:, i2 * 2:(i2 + 1) * 2, :],
                           PS[b][:].rearrange("p (a b) -> p a b", a=2), reads=[('ps', b)], writes=[kv])
                    cpi[0] += 1
                P.dma('sp', dest[:, sl_ * 256:(sl_ + 1) * 256].rearrange("(a p) d -> p a d", p=128), vt[:],
                      reads=[kv], writes=[('dram', sname)])
        P.barrier()
        P.emit()
    if stop <= 1:
        return

    with ExitStack() as esO:
        OTA = sb(nc, esO, "a_OTA", [128, 16, T], BF16)
        esN = ExitStack()
        with esN as es:
            ident = CB.load(P, nc, es, blob_d, 'ident', BF16)
            KCT = sb(nc, es, "a_KCT", [128, 4, 128], BF16)
            VCX = sb(nc, es, "a_VCX", [128, 4, 161], BF16)
            GTM = sb(nc, es, "a_GTM", [128, NI, 48], F32)
            P.dma('sp', GTM[:].rearrange("p a b -> p (a b)"), GTMd, writes=['GTM'])
            with ExitStack() as esc:
                rotm = CB.load(P, nc, esc, blob_d, 'rotm', BF16)
                cosc = CB.load(P, nc, esc, blob_d, 'cosc', F32, q='sp')
                sinc = CB.load(P, nc, esc, blob_d, 'sinc', F32, q='sp')
                wk = sb(nc, esc, "a_wk", [128, 32, 128], BF16)
                wv = sb(nc, esc, "a_wv", [128, 32, 128], BF16)
                P.dma('pool', wk[:], cmp_wk.rearrange("l d e -> d l e"), writes=['wk'])
                P.dma('pool', wv[:], cmp_wv.rearrange("l d e -> d l e"), writes=['wv'])
                pek = sb(nc, esc, "a_pek", [128, 32], F32)
                pev = sb(nc, esc, "a_pev", [128, 32], F32)
                P.dma('sp', pek[:], pekT, writes=['pek'])
                P.dma('sp', pev[:], pevT, writes=['pev'])
                krr = Rot('krr', [sb(nc, esc, "a_kr%d" % i, [128, T], BF16) for i in range(2)])
                vrr = Rot('vrr', [sb(nc, esc, "a_vr%d" % i, [128, T], BF16) for i in range(2)])
                blk = Rot('blk', [sb(nc, esc, "a_blk%d" % i, [128, 128], BF16) for i in range(4)])
                craw = sb(nc, esc, "a_craw", [128, 128], BF16)
                ct1 = sb(nc, esc, "a_ct1", [128, 128], F32)
                ct2 = sb(nc, esc, "a_ct2", [128, 128], F32)
                o_, n_, _sh = CB.off['vcx']
                for k in range(4):
                    P.dma('pool', VCX[:, k, 128:161], blob_d[:, o_:o_ + n_], writes=[('VCXc', k)])
                for k in range(4):
                    kr, kkr = krr.next()
                    vr, kvr = vrr.next()
                    P.dma('sp', kr[:], KCR[k], writes=[kkr])
                    P.dma('sp', vr[:], VCR[k], writes=[kvr])
                    bK = 0
                    bV = 1
                    for (src, ksrc, pe_, kpe, isk) in ((kr, kkr, pek, 'pek', True), (vr, kvr, pev, 'pev', False)):
                        v3 = src[:].rearrange("p (n s) -> p n s", s=16)
                        for l in range(32):
                            bl, kbl = blk.next()
                            view = v3[:, (l // 16):(l // 16) + NCMP, l % 16]
                            P.ts('dve', bl[:, 0:NCMP], view, pe_[:, l:l + 1], None, ALU.add, None,
                                 reads=[ksrc, kpe], writes=[kbl])
                            if isk:
                                P.mm(PS[bK][:, 0:NCMP], wk[:, l, :], bl[:, 0:NCMP], l == 0, l == 31,
                                     reads=['wk', kbl], writes=[('ps', bK)], inc=True)
                            else:
                                P.mm(PS[bV][0:NCMP, 0:128], bl[:, 0:NCMP], wv[:, l, :], l == 0, l == 31,
                                     reads=['wv', kbl], writes=[('ps', bV)], inc=True)
                    P.copy('act', craw[:, 0:NCMP], PS[bK][:, 0:NCMP], reads=[('ps', bK)], writes=['craw'])
                    b2 = 2
                    P.mm(PS[b2][:, 0:NCMP], rotm[:], craw[:, 0:NCMP], True, True,
                         reads=['craw', ('const', 'rotm')], writes=[('ps', b2)], inc=True)
                    P.tt('dve', ct1[:, 0:NCMP], PS[bK][:, 0:NCMP], cosc[:], ALU.mult,
                         reads=[('ps', bK), ('const', 'cosc')], writes=['ct1'])
                    P.tt('dve', ct2[:, 0:NCMP], PS[b2][:, 0:NCMP], sinc[:], ALU.mult,
                         reads=[('ps', b2), ('const', 'sinc')], writes=['ct2'])
                    P.tt('dve', KCT[:, k, 0:NCMP], ct1[:, 0:NCMP], ct2[:, 0:NCMP], ALU.add,
                         reads=['ct1', 'ct2'], writes=[('KCT', k)])
                    P.copy('act', VCX[0:NCMP, k, 0:128], PS[bV][0:NCMP, 0:128], reads=[('ps', bV)],
                           writes=[('VCX', k)])
                P.barrier()
                P.emit()
            with ExitStack() as esn:
                if stop <= 2:
                    return
                identf = CB.load(P, nc, esn, blob_d, 'ident', F32, q='sp')
                negw = CB.load(P, nc, esn, blob_d, 'negw', BF16)
                negvalid = CB.load(P, nc, esn, blob_d, 'negvalid', BF16)
                expm = CB.load(P, nc, esn, blob_d, 'expm', BF16)
                vnf = CB.load(P, nc, esn, blob_d, 'vnf', F32, q='sp')
                addc = CB.load(P, nc, esn, blob_d, 'addc', F32, q='sp')
                QAg = [sb(nc, esn, "a_qa%d" % g, [128, T], BF16) for g in range(4)]
                KSk = sb(nc, esn, "a_ksk", [128, T], BF16)
                KWk = sb(nc, esn, "a_kwk", [128, T], BF16)
                VSk = sb(nc, esn, "a_vsk", [128, NI, 129], BF16)
                VWk = sb(nc, esn, "a_vwk", [128, NI, 129], BF16)
                OA = [sb(nc, esn, "a_oa%d" % g, [128, NI, 128], F32) for g in range(4)]
                IMP = sb(nc, esn, "a_imp", [128, NI, 32], F32)
                IMF = sb(nc, esn, "a_imf", [128, NI, 32], F32)
                IM2 = sb(nc, esn, "a_im2", [128, NI, 32], F32)
                M8 = sb(nc, esn, "a_m8", [128, NI, 8], F32)
                M8b = sb(nc, esn, "a_m8b", [128, NI, 8], F32)
                NSL = sb(nc, esn, "a_nsl", [128, NI, 32], BF16)
                NEGSEL = sb(nc, esn, "a_negsel", [32, T], BF16)
                CH = []
                for c in range(2):
                    CH.append(dict(
                        c=c, sbk=[4 * c, 4 * c + 1], ob=(4 * c + 2, 4 * c + 3), si=[0],
                        ET=Rot('et%d' % c, [sb(nc, esn, "a_et%d_%d" % (c, i), [128, 512], BF16) for i in range(3)]),
                        ORW=Rot('orw%d' % c, [sb(nc, esn, "a_orw%d_%d" % (c, i), [128, 4, 161], F32) for i in range(2)]),
                        RZ=sb(nc, esn, "a_rz%d" % c, [128, 4, 1], F32), SC=sb(nc, esn, "a_sc%d" % c, [128, 4, 1], F32),
                        TMPO=sb(nc, esn, "a_tmpo%d" % c, [128, 4, 128], F32),
                        TMPU=sb(nc, esn, "a_tmpu%d" % c, [128, 4, 32], F32)))
                P.ins('dve', lambda eng: eng.memset(VSk[:, :, 128:129], 1.0), writes=['VSk1'])
                P.ins('dve', lambda eng: eng.memset(VWk[:, :, 128:129], 1.0), writes=['VWk1'])
                cpj = [0]

                def interleave(gens):
                    gens = list(gens)
                    while gens:
                        for g_ in list(gens):
                            try:
                                next(g_)
                            except StopIteration:
                                gens.remove(g_)

                def chain_gen(tasks):
                    for t_ in tasks:
                        yield from t_

                def finish_tt(ch, g, tt, gate_col, first, with_imp):
                    c = ch['c']
                    W = 161 if with_imp else 129
                    orw, korw = ch['ORW'].next()
                    RZ, SC, TMPO, TMPU = ch['RZ'], ch['SC'], ch['TMPO'], ch['TMPU']
                    for half in range(2):
                        bo = ch['ob'][half]
                        P.copy('dve', orw[:, 2 * half:2 * half + 2, 0:W],
                               PS[bo][:, 0:2 * W].rearrange("p (a b) -> p a b", a=2), reads=[('ps', bo)], writes=[korw])
                    P.ts('dve', RZ[:], orw[:, :, 128:129], 1e-30, None, ALU.max, None, reads=[korw], writes=[('RZ', c)])
                    P.ins('dve', lambda eng: eng.reciprocal(RZ[:], RZ[:]), reads=[('RZ', c)], writes=[('RZ', c)])
                    i0 = 4 * tt
                    if with_imp:
                        P.tt('dve', TMPU[:], orw[:, :, 129:161], RZ[:].to_broadcast([128, 4, 32]), ALU.mult,
                             reads=[korw, ('RZ', c)], writes=[('TMPU', c)])
                        P.tt('dve', IMP[:, i0:i0 + 4, :], IMP[:, i0:i0 + 4, :], TMPU[:], ALU.add,
                             reads=[('IMP', tt), ('TMPU', c)], writes=[('IMP', tt)])
                    P.tt('dve', SC[:], RZ[:], GTM[:, i0:i0 + 4, gate_col:gate_col + 1], ALU.mult,
                         reads=[('RZ', c), 'GTM'], writes=[('SC', c)])
                    if first:
                        P.tt('pool', OA[g][:, i0:i0 + 4, :], orw[:, :, 0:128], SC[:].to_broadcast([128, 4, 128]), ALU.mult,
                             reads=[korw, ('SC', c)], writes=[('OA', g, tt)])
                    else:
                        P.tt('dve', TMPO[:], orw[:, :, 0:128], SC[:].to_broadcast([128, 4, 128]), ALU.mult,
                             reads=[korw, ('SC', c)], writes=[('TMPO', c)])
                        P.tt('pool', OA[g][:, i0:i0 + 4, :], OA[g][:, i0:i0 + 4, :], TMPO[:], ALU.add,
                             reads=[('OA', g, tt), ('TMPO', c)], writes=[('OA', g, tt)])

                def s_bank(ch):
                    b = ch['sbk'][ch['si'][0] % 2]
                    ch['si'][0] += 1
                    return b

                def cmp_task(ch, k, g):
                    h = 4 * k + g
                    for tt in range(NT):
                        tsl = slice(tt * 512, (tt + 1) * 512)
                        b = s_bank(ch)
                        P.mm(PS[b][0:NCMP, :], KCT[:, k, 0:NCMP], QAg[g][:, tsl], True, False,
                             reads=[('KCT', k), ('QAg', g)], writes=[('ps', b)], inc=False)
                        P.mm(PS[b][0:NCMP, :], ident[0:NCMP, 0:NCMP], negvalid[0:NCMP, tsl], False, True,
                             reads=[('const', 'ident'), ('const', 'negvalid')], writes=[('ps', b)], inc=True)
                        et, ket = ch['ET'].next()
                        P.act(et[0:NCMP, :], PS[b][0:NCMP, :], AF.Exp, reads=[('ps', b)], writes=[ket], scale=SCALE)
                        for cc in range(4):
                            bo = ch['ob'][cc // 2]
                            o0 = (cc % 2) * 161
                            P.mm(PS[bo][:, o0:o0 + 161], et[0:NCMP, cc * 128:(cc + 1) * 128], VCX[0:NCMP, k, :],
                                 True, True, reads=[ket, ('VCX', k), ('VCXc', k)], writes=[('ps', bo)], inc=(cc % 2 == 1))
                        finish_tt(ch, g, tt, 0 * 16 + h, True, True)
                        yield

                def att_task(ch, k, g, brn):
                    h = 4 * k + g
                    if brn == 'slc':
                        Kk, kK, Vk, kV, gbase = KSk, 'KSk', VSk, ('VSk', 'VSk1'), 16
                    else:
                        Kk, kK, Vk, kV, gbase = KWk, 'KWk', VWk, ('VWk', 'VWk1'), 32
                    for tt in range(NT):
                        tsl = slice(tt * 512, (tt + 1) * 512)
                        a_lo = 0 if brn == 'slc' else max(0, 4 * tt - 4)
                        a_hi = 4 * tt + 3
                        first_a = {}
                        last_a = {}
                        started = {}
                        for cc in range(4):
                            first_a[cc] = 0 if brn == 'slc' else max(a_lo, 4 * tt + cc - 4)
                            last_a[cc] = 4 * tt + cc

                        def pv(a, et, ket):
                            cs_ = [cc for cc in range(4) if first_a[cc] <= a <= last_a[cc]]
                            for cc in cs_:
                                bo = ch['ob'][cc // 2]
                                o0 = (cc % 2) * 129
                                P.mm(PS[bo][:, o0:o0 + 129], et[:, cc * 128:(cc + 1) * 128], Vk[:, a, :],
                                     bo not in started, a == last_a[cc], reads=[ket, kV[0], kV[1]],
                                     writes=[('ps', bo)], inc=(cc == cs_[-1]))
                                started[bo] = True
                        prev = None
                        for a in range(a_lo, a_hi + 1):
                            r = a - 4 * tt
                            b = s_bank(ch)
                            need_w = (brn == 'win') or (r >= 0)
                            P.mm(PS[b][:], Kk[:, a * 128:(a + 1) * 128], QAg[g][:, tsl], True, False,
                                 reads=[kK, ('QAg', g)], writes=[('ps', b)], inc=False)
                            if brn == 'slc':
                                P.mm(PS[b][:], expm[0:32, a, :], NEGSEL[:, tsl], False, not need_w,
                                     reads=[('const', 'expm'), 'NEGSEL'], writes=[('ps', b)], inc=not need_w)
                            if need_w:
                                P.mm(PS[b][:], ident[:], negw[:, r + 4, :], False, True,
                                     reads=[('const', 'ident'), ('const', 'negw')], writes=[('ps', b)], inc=True)
                            et, ket = ch['ET'].next()
                            P.act(et[:], PS[b][:], AF.Exp, reads=[('ps', b)], writes=[ket], scale=SCALE)
                            if prev is not None:
                                pv(*prev)
                            prev = (a, et, ket)
                            yield
                        pv(*prev)
                        finish_tt(ch, g, tt, gbase + h, False, False)
                        yield

                for k in range(4):
                    for g in range(4):
                        P.dma('sp', QAg[g][:], QA[4 * k + g], writes=[('QAg', g)])
                    P.dma('sp', KSk[:], KS[k], writes=['KSk'])
                    P.dma('sp', KWk[:], KW[k], writes=['KWk'])
                    P.dma('sp', VSk[:, :, 0:128], VS[:, k * 128:(k + 1) * 128].rearrange("(a p) d -> p a d", p=128),
                          writes=['VSk'])
                    P.dma('sp', VWk[:, :, 0:128], VW[:, k * 128:(k + 1) * 128].rearrange("(a p) d -> p a d", p=128),
                          writes=['VWk'])
                    P.ins('dve', lambda eng: eng.memset(IMP[:], 0.0), writes=[('IMP', tt) for tt in range(NT)])
                    interleave([chain_gen([cmp_task(CH[0], k, 0), cmp_task(CH[0], k, 1)]),
                                chain_gen([cmp_task(CH[1], k, 2), cmp_task(CH[1], k, 3)])])
                    P.tt('dve', IMF[:], IMP[:], vnf[:], ALU.mult, reads=[('IMP', tt) for tt in range(NT)] + [('const', 'vnf')],
                         writes=['IMF'])
                    P.tt('dve', IMF[:], IMF[:], addc[:], ALU.add, reads=['IMF', ('const', 'addc')], writes=['IMF'])
                    for i in range(NI):
                        P.ins('dve', lambda eng, i=i: eng.max(M8[:, i, :], IMF[:, i, :]), reads=['IMF'],
                              writes=[('M8', i)])
                        P.ins('dve', lambda eng, i=i: eng.match_replace(IM2[:, i, :], M8[:, i, :], IMF[:, i, :], -3.0e38),
                              reads=['IMF', ('M8', i)], writes=[('IM2', i)])
                        P.ins('dve', lambda eng, i=i: eng.max(M8b[:, i, :], IM2[:, i, :]), reads=[('IM2', i)],
                              writes=[('M8b', i)])
                        P.ts('dve', NSL[:, i, :], IMF[:, i, :], M8b[:, i, 7:8], -BIG, ALU.is_lt, ALU.mult,
                             reads=['IMF', ('M8b', i)], writes=[('NSL', i)])
                    for half in range(2):
                        b = 0
                        pb = PS[b][:].bitcast(BF16)
                        for m in range(8):
                            i = half * 8 + m
                            P.transpose(pb[0:32, m * 128:(m + 1) * 128], NSL[:, i, :], ident[:],
                                        reads=[('NSL', i), ('const', 'ident')], writes=[('ps', b)], inc=(m == 7))
                        P.copy('dve', NEGSEL[:, half * 1024:(half + 1) * 1024], pb[0:32, :], reads=[('ps', b)],
                               writes=['NEGSEL'])
                    interleave([chain_gen([att_task(CH[0], k, 0, 'slc'), att_task(CH[0], k, 1, 'slc'),
                                           att_task(CH[0], k, 0, 'win'), att_task(CH[0], k, 1, 'win')]),
                                chain_gen([att_task(CH[1], k, 2, 'slc'), att_task(CH[1], k, 3, 'slc'),
                                           att_task(CH[1], k, 2, 'win'), att_task(CH[1], k, 3, 'win')])])
                    for g in range(4):
                        h = 4 * k + g
                        for i4 in range(NI // 4):
                            b = cpj[0] % 2
                            for m in range(4):
                                i = i4 * 4 + m
                                P.transpose(PS[b][:, m * 128:(m + 1) * 128], OA[g][:, i, :], identf[:],
                                            reads=[('OA', g, i4), ('const', 'ident')], writes=[('ps', b)], inc=(m == 3))
                            P.copy('act' if cpj[0] % 2 == 0 else 'dve', OTA[:, h, i4 * 512:(i4 + 1) * 512], PS[b][:],
                                   reads=[('ps', b)], writes=[('OT', h)])
                            cpj[0] += 1
                P.barrier()
                P.emit()
        if stop <= 3:
            return
        OTB = sb(nc, esO, "a_OTB", [128, 16, T], BF16)
        if True:
            with ExitStack() as ess:
                ident = CB.load(P, nc, ess, blob_d, 'ident', BF16)
                tri = CB.load(P, nc, ess, blob_d, 'tri', BF16)
                omt = CB.load(P, nc, ess, blob_d, 'onesmtri', BF16)
                negsb = CB.load(P, nc, ess, blob_d, 'negsb', BF16)
                QBh = Rot('QBh', [sb(nc, ess, "a_qb%d" % i, [128, T], BF16) for i in range(2)])
                KBh = Rot('KBh', [sb(nc, ess, "a_kb%d" % i, [128, T], BF16) for i in range(2)])
                VBh = Rot('VBh', [sb(nc, ess, "a_vb%d" % i, [128, NI, 128], BF16) for i in range(2)])
                CHS = []
                for c in range(2):
                    CHS.append(dict(
                        c=c, sbk=[4 * c, 4 * c + 1], X=4 * c + 2, O=4 * c + 3, si=[0],
                        SP=Rot('sbsp%d' % c, [sb(nc, ess, "a_ssp%d_%d" % (c, i), [128, 512], F32) for i in range(2)]),
                        SPB=Rot('sbspb%d' % c, [sb(nc, ess, "a_sspb%d_%d" % (c, i), [128, 512], BF16) for i in range(4)]),
                        U=Rot('sbu%d' % c, [sb(nc, ess, "a_su%d_%d" % (c, i), [128, 512], F32) for i in range(4)]),
                        A=Rot('sba%d' % c, [sb(nc, ess, "a_sa%d_%d" % (c, i), [128, 512], BF16) for i in range(3)])))
                LA = 3

                def interleave2(gens):
                    gens = list(gens)
                    while gens:
                        for g_ in list(gens):
                            try:
                                next(g_)
                            except StopIteration:
                                gens.remove(g_)

                def load_head(h):
                    q, kq = QBh.next()
                    kk, kkk = KBh.next()
                    v, kv = VBh.next()
                    P.dma('sp', q[:], QB[h], writes=[kq])
                    P.dma('sp', kk[:], KB[h], writes=[kkk])
                    P.dma('sp', v[:], VB[:, h * 128:(h + 1) * 128].rearrange("(a p) d -> p a d", p=128), writes=[kv])
                    return (q, kq, kk, kkk, v, kv)

                def sb_sweep(ch, head, h, tt):
                    (q, kq, kk, kkk, v, kv) = head
                    tsl = slice(tt * 512, (tt + 1) * 512)
                    amax = 4 * tt + 3
                    bX = ch['X']
                    bO = ch['O']
                    st = {}

                    def stage1(a):
                        r = a - 4 * tt
                        b = ch['sbk'][ch['si'][0] % 2]
                        ch['si'][0] += 1
                        P.mm(PS[b][:], kk[:, a * 128:(a + 1) * 128], q[:, tsl], True, r < 0,
                             reads=[kkk, kq], writes=[('ps', b)], inc=(r < 0))
                        if r >= 0:
                            P.mm(PS[b][:], ident[:], negsb[:, r, :], False, True,
                                 reads=[('const', 'ident'), ('const', 'negsb')], writes=[('ps', b)], inc=True)
                        sp, ksp = ch['SP'].next()
                        spb, kspb = ch['SPB'].next()
                        u, ku = ch['U'].next()
                        P.act(sp[:], PS[b][:], AF.Exp, reads=[('ps', b)], writes=[ksp], scale=SCALE)
                        P.act(sp[:], sp[:], AF.Ln, reads=[ksp], writes=[ksp], bias=1.0)
                        P.copy('pool' if ch['c'] == 1 else 'dve', spb[:], sp[:], reads=[ksp], writes=[kspb])
                        P.stt('dve', u[:], PS[b][:], SCALE, sp[:], ALU.mult, ALU.subtract,
                              reads=[('ps', b), ksp], writes=[ku])
                        st[a] = (spb, kspb, u, ku)

                    def pv(a, A, kA):
                        P.mm(PS[bO][:], v[:, a, :], A[:], a == amax, a == 0, reads=[kv, kA],
                             writes=[('ps', bO)], inc=True)
                    nxt1 = amax
                    for _ in range(LA):
                        if nxt1 >= 0:
                            stage1(nxt1)
                            nxt1 -= 1
                    yield
                    prevA = None
                    for a in range(amax, -1, -1):
                        spb, kspb, u, ku = st.pop(a)
                        P.mm(PS[bX][:], tri[:], spb[:], a == amax, a == 0, reads=[('const', 'tri'), kspb],
                             writes=[('ps', bX)], inc=True)
                        P.tt('dve', u[:], u[:], PS[bX][:], ALU.subtract, reads=[ku, ('ps', bX)], writes=[ku])
                        yield
                        if nxt1 >= 0:
                            stage1(nxt1)
                            nxt1 -= 1
                        yield
                        if a > 0:
                            P.mm(PS[bX][:], omt[:], spb[:], False, False, reads=[('const', 'onesmtri'), kspb],
                                 writes=[('ps', bX)], inc=True)
                        A, kA = ch['A'].next()
                        P.act(A[:], u[:], AF.Exp, reads=[ku], writes=[kA])
                        if prevA is not None:
                            pv(*prevA)
                        prevA = (a, A, kA)
                        yield
                    pv(*prevA)
                    P.copy('dve', OTB[:, h, tsl], PS[bO][:], reads=[('ps', bO)], writes=[('OT', 16 + h)])
                    yield

                def sb_chain(ch, head, h, tts):
                    for tt in tts:
                        yield from sb_sweep(ch, head, h, tt)
                nxt = load_head(0)
                for h in range(16):
                    head = nxt
                    if h + 1 < 16:
                        nxt = load_head(h + 1)
                    interleave2([sb_chain(CHS[0], head, h, [3, 0]), sb_chain(CHS[1], head, h, [2, 1])])
                P.barrier()
                P.emit()
        if stop <= 4:
            return
        with ExitStack() as es:
            wov = w_out.rearrange("(c p) n -> p c n", p=128)
            wo = Rot('wo', [sb(nc, es, "a_wo%d" % i, [128, 32, 512], BF16) for i in range(2)])
            xr = Rot('xr', [sb(nc, es, "a_xr%d" % i, [128, 512], F32) for i in range(2)])
            yo = Rot('yo', [sb(nc, es, "a_yo%d" % i, [128, 512], F32) for i in range(2)])
            if STd is not None:
                ST = sb(nc, es, "a_ST", [128, NI, D // 512, 6], F32)
            for db in range(D // 512):
                w, kw_ = wo.next()
                P.dma('pool', w[:], wov[:, :, db * 512:(db + 1) * 512], writes=[kw_])
                for i in range(NI):
                    x_, kx = xr.next()
                    y_, ky = yo.next()
                    P.dma('act', x_[:], x_in[i * 128:(i + 1) * 128, db * 512:(db + 1) * 512], writes=[kx])
                    b = P.next_bank()
                    for c in range(32):
                        P.mm(PS[b][:], (OTA if c < 16 else OTB)[:, c % 16, i * 128:(i + 1) * 128], w[:, c, :], c == 0, c == 31,
                             reads=[kw_], writes=[('ps', b)], inc=(c == 31))
                    P.stt('dve', y_[:], x_[:], ALPHA, PS[b][:], ALU.mult, ALU.add, reads=[kx, ('ps', b)], writes=[ky])
                    if STd is not None:
                        P.ins('dve', lambda eng, o=ST[:, i, db, :], a=y_[:]: eng.bn_stats(o, a),
                              reads=[ky], writes=[('ST', i, db)])
                    P.dma('sp', Y[i * 128:(i + 1) * 128, db * 512:(db + 1) * 512], y_[:], reads=[ky], writes=['Y'])
            if STd is not None:
                P.dma('sp', STd, ST[:].rearrange("p a b c -> p (a b c)"),
                      reads=[('ST', i, db) for i in range(NI) for db in range(D // 512)], writes=['STd'])
            P.barrier()
            P.emit()
    o_i, n_i, _ = CB.off['ident']
    return final_ln_phase(P, nc, PS, blob_d[:, o_i:o_i + n_i], Y, x_out, lng, lnb, T, D, "al", want_xt, STd=STd)


def attn_scratch(nc, T, D, kind="Internal"):
    S = {}
    S['QA'] = nc.dram_tensor("s_QA", [16, 128, T], BF16, kind=kind).ap()
    S['KCR'] = nc.dram_tensor("s_KCR", [4, 128, T], BF16, kind=kind).ap()
    S['VCR'] = nc.dram_tensor("s_VCR", [4, 128, T], BF16, kind=kind).ap()
    S['KS'] = nc.dram_tensor("s_KS", [4, 128, T], BF16, kind=kind).ap()
    S['KW'] = nc.dram_tensor("s_KW", [4, 128, T], BF16, kind=kind).ap()
    S['QB'] = nc.dram_tensor("s_QB", [16, 128, T], BF16, kind=kind).ap()
    S['KB'] = nc.dram_tensor("s_KB", [16, 128, T], BF16, kind=kind).ap()
    S['VS'] = nc.dram_tensor("s_VS", [T, 512], BF16, kind=kind).ap()
    S['VW'] = nc.dram_tensor("s_VW", [T, 512], BF16, kind=kind).ap()
    S['VB'] = nc.dram_tensor("s_VB", [T, 2048], BF16, kind=kind).ap()
    S['GTM'] = nc.dram_tensor("s_GTM", [128, (T // 128) * 48], F32, kind=kind).ap()
    S['Y'] = nc.dram_tensor("s_Ya", [T, D], F32, kind=kind).ap()
    return S


T_SEQ = 2048
D_MODEL = 4096
FC_FF = 86
FUSED = True
_CB = None
_PROG_CACHE = {}


def get_cb():
    global _CB
    if _CB is None:
        c = attn_consts()
        c['poolc'] = pool_consts()
        _CB = ConstBlob(c)
    return _CB


def build_program(stages):
    T, D, FC = T_SEQ, D_MODEL, FC_FF
    F = FC * 128
    CB = get_cb()
    nc = bass.Bass("TRN2", target_bir_lowering=False)

    def ein(name, shape):
        return nc.dram_tensor(name, list(shape), F32, kind="ExternalInput").ap()
    x = ein("x", [T, D])
    blob = ein("blob", [128, CB.ncols])
    y = nc.dram_tensor("y", [T, D], F32, kind="ExternalOutput").ap()
    io = {}
    if 'attn' in stages:
        io['attn'] = dict(w_in=ein("w_in", [D, IN_DIM]), w_out=ein("w_out", [4096, D]), cwk=ein("cwk", [32, 128, 128]),
                          cwv=ein("cwv", [32, 128, 128]), pekT=ein("pekT", [128, 32]), pevT=ein("pevT", [128, 32]),
                          lng=ein("lng_a", [1, D]), lnb=ein("lnb_a", [1, D]))
    for l in (0, 1):
        if 'ffn%d' % l in stages:
            io['ffn%d' % l] = dict(w_up=ein("w_up%d" % l, [D, 2 * F]), convp=ein("convp%d" % l, [128, 8 * FC]),
                                   w_down=ein("w_down%d" % l, [F, D]), lng=ein("lng_f%d" % l, [1, D]),
                                   lnb=ein("lnb_f%d" % l, [1, D]))
    if 'pool' in stages:
        io['pool'] = dict(pool_w=ein("pool_w", [4, D // 4, D // 4]), pool_scale=ein("pool_scale", [1, D]),
                          lng=ein("lng_p", [1, D]), lnb=ein("lnb_p", [1, D]))
    Yd = nc.dram_tensor("s_Y", [T, D], F32).ap()
    STd = nc.dram_tensor("s_ST", [128, (T // 128) * (D // 512) * 6], F32).ap()
    G = None
    if 'ffn0' in stages or 'ffn1' in stages:
        G = nc.dram_tensor("s_G", [T // 128, 128, FC, 128], BF16).ap()
    S = None
    if 'attn' in stages:
        S = attn_scratch(nc, T, D)
    cur = x
    o_id, n_id, _ = CB.off['ident']
    o_pc, n_pc, _ = CB.off['poolc']
    with ExitStack() as es:
        P = Prog(nc, es)
        PS = [es.enter_context(nc.psum_tensor("ps%d" % i, [128, 512], F32)) for i in range(8)]
        xt = None
        for si, st in enumerate(stages):
            dst = y if si == len(stages) - 1 else nc.dram_tensor("s_X%d" % si, [T, D], F32).ap()
            a = io[st]
            nxt_ffn = si + 1 < len(stages) and stages[si + 1].startswith('ffn')
            if st == 'attn':
                xt = stage_attn(P, nc, PS, CB, blob, cur, dst, a['w_in'], a['w_out'], a['cwk'], a['cwv'], a['pekT'],
                                a['pevT'], a['lng'], a['lnb'], S, T, D, want_xt=nxt_ffn, STd=STd)
            elif st == 'pool':
                xt = stage_pool(P, nc, PS, cur, dst, a['pool_w'], a['pool_scale'],
                                blob[:, o_pc:o_pc + n_pc].rearrange("p (a b) -> p a b", a=12), a['lng'], a['lnb'], Yd, T, D,
                                ident_d=blob[:, o_id:o_id + n_id], want_xt=nxt_ffn, STd=STd)
            else:
                stage_ffn(P, nc, PS, blob[:, o_id:o_id + n_id], cur, dst, a['w_up'], a['convp'], a['w_down'],
                          a['lng'], a['lnb'], G, Yd, T, D, FC, xt_in=xt, STd=STd)
                xt = None
            cur = dst
    return nc


def _prog(stages):
    key = tuple(stages)
    if key not in _PROG_CACHE:
        _PROG_CACHE[key] = build_program(list(stages))
    return _PROG_CACHE[key]


def _f32(a):
    return np.ascontiguousarray(np.asarray(a, dtype=np.float32))


def kernel(x, attn_w_in, attn_w_out, cmp_w_k, cmp_w_v, cmp_pe_k, cmp_pe_v, pool_w, pool_scale,
           ffn_w_up, ffn_conv_w, ffn_conv_b, ffn_w_down, ln_mix_g, ln_mix_b, ln_ffn_g, ln_ffn_b):
    n = 8
    CB = get_cb()
    x = _f32(x)
    FC = FC_FF
    shared = {}
    shared['attn'] = dict(w_in=_f32(attn_w_in)[0], w_out=_f32(attn_w_out)[0], cwk=_f32(cmp_w_k)[0], cwv=_f32(cmp_w_v)[0],
                          pekT=np.ascontiguousarray(_f32(cmp_pe_k)[0].T), pevT=np.ascontiguousarray(_f32(cmp_pe_v)[0].T),
                          lng_a=_f32(ln_mix_g)[0:1], lnb_a=_f32(ln_mix_b)[0:1])
    for l in (0, 1):
        cw4 = np.concatenate([_f32(ffn_conv_w)[l], _f32(ffn_conv_b)[l][None]], 0)
        convp = np.ascontiguousarray(cw4.reshape(4, 2 * FC, 128).transpose(2, 0, 1)).reshape(128, 8 * FC)
        shared['ffn%d' % l] = {"w_up%d" % l: _f32(ffn_w_up)[l], "convp%d" % l: convp, "w_down%d" % l: _f32(ffn_w_down)[l],
                               "lng_f%d" % l: _f32(ln_ffn_g)[l:l + 1], "lnb_f%d" % l: _f32(ln_ffn_b)[l:l + 1]}
    shared['pool'] = dict(pool_w=_f32(pool_w)[0], pool_scale=_f32(pool_scale)[0:1],
                          lng_p=_f32(ln_mix_g)[1:2], lnb_p=_f32(ln_mix_b)[1:2])
    order = ['attn', 'ffn0', 'pool', 'ffn1']
    groups = [order] if FUSED else [[s] for s in order]
    cur = [x[c] for c in range(n)]
    for stages in groups:
        nc = _prog(stages)
        base = {"blob": CB.arr}
        for s in stages:
            base.update(shared[s])
        in_maps = []
        for c in range(n):
            m = dict(base)
            m["x"] = cur[c]
            in_maps.append(m)
        res = run_bass_kernel_spmd(nc, in_maps, core_ids=list(range(n)))
        cur = [np.asarray(res.results[c]["y"]) for c in range(n)]
    return np.stack(cur, 0).astype(np.float32)
```

```python
import numpy as np
import ml_dtypes
from contextlib import ExitStack
import concourse.bass as bass
import concourse.mybir as mybir
from concourse.bass_utils import run_bass_kernel_spmd

F32 = mybir.dt.float32
BF16 = mybir.dt.bfloat16
AF = mybir.ActivationFunctionType
ALU = mybir.AluOpType
AX = mybir.AxisListType

ENGS = ('sp', 'act', 'pe', 'dve', 'pool')
ALPHA = float((2 * 2) ** 0.25)
LN_EPS = 1e-5


class Prog:
    def __init__(self, nc, es, ndma=10):
        self.nc = nc
        self.ndma = ndma
        self.sems = {}
        self.val = {}
        for e in ENGS:
            self.sems[e] = es.enter_context(nc.semaphore("c_" + e))
            self.val[e] = 0
        for q in ('sp', 'act', 'pool'):
            for i in range(ndma):
                k = (q, i)
                self.sems[k] = es.enter_context(nc.semaphore("d_%s%d" % (q, i)))
                self.val[k] = 0
        self.dnext = {'sp': 0, 'act': 0, 'pool': 0}
        self.seen = {e: {} for e in ENGS}
        self.lastw = {}
        self.readers = {}
        self.prog = {e: [] for e in ENGS}
        self.pending = {e: [] for e in ENGS}
        self.pend_keys = {}
        self.bank = 0
        self.ninstr = 0

    def _need(self, e, reads, writes):
        need = {}
        for r in reads:
            pk = self.pend_keys.get(r)
            if pk is not None and pk[1] and pk[0] != e:
                raise RuntimeError("dependency on pending write %r" % (r,))
            ev = self.lastw.get(r)
            if ev is not None and need.get(ev[0], 0) < ev[1]:
                need[ev[0]] = ev[1]
        for w in writes:
            pk = self.pend_keys.get(w)
            if pk is not None and pk[0] != e:
                raise RuntimeError("dependency on pending access %r" % (w,))
            ev = self.lastw.get(w)
            if ev is not None and need.get(ev[0], 0) < ev[1]:
                need[ev[0]] = ev[1]
            rd = self.readers.get(w)
            if rd:
                for k, v in rd.items():
                    if need.get(k, 0) < v:
                        need[k] = v
        return need

    def _sync(self, e, reads, writes):
        need = self._need(e, reads, writes)
        seen = self.seen[e]
        for k, v in need.items():
            if k == e and e == 'pe':
                continue
            if seen.get(k, 0) >= v:
                continue
            self.prog[e].append(('w', k, v))
            seen[k] = v

    def _record(self, ev, reads, writes):
        for w in writes:
            self.lastw[w] = ev
            self.readers[w] = {}
        for r in reads:
            d = self.readers.get(r)
            if d is None:
                d = self.readers[r] = {}
            if d.get(ev[0], 0) < ev[1]:
                d[ev[0]] = ev[1]

    def ins(self, e, build, reads=(), writes=(), inc=True):
        self.ninstr += 1
        pr = [r for r in reads if isinstance(r, tuple) and r[0] == 'ps']
        if pr:
            reads = [r for r in reads if not (isinstance(r, tuple) and r[0] == 'ps')]
            writes = list(writes) + [r for r in pr if r not in writes]
        self._sync(e, reads, writes)
        if inc:
            self.val[e] += 1
            ev = (e, self.val[e])
            self.prog[e].append(('i', build, e, 1))
            self._record(ev, reads, writes)
            if self.pending[e]:
                for (r, w) in self.pending[e]:
                    self._record(ev, r, w)
                    for k in r:
                        self.pend_keys.pop(k, None)
                    for k in w:
                        self.pend_keys.pop(k, None)
                self.pending[e] = []
        else:
            self.prog[e].append(('i', build, None, 0))
            self.pending[e].append((tuple(reads), tuple(writes)))
            for k in reads:
                if k not in self.pend_keys:
                    self.pend_keys[k] = (e, False)
            for k in writes:
                self.pend_keys[k] = (e, True)

    def dma(self, q, out, in_, reads=(), writes=()):
        self.ninstr += 1
        i = self.dnext[q]
        self.dnext[q] = (i + 1) % self.ndma
        k = (q, i)
        if self.val[k] > 0 and self.seen[q].get(k, 0) < self.val[k]:
            self.prog[q].append(('w', k, self.val[k]))
            self.seen[q][k] = self.val[k]
        self._sync(q, reads, writes)
        self.val[k] += 16
        ev = (k, self.val[k])
        self.prog[q].append(('i', lambda eng, o=out, i_=in_: eng.dma_start(out=o, in_=i_), k, 16))
        self._record(ev, reads, writes)

    def barrier(self):
        for e in ENGS:
            assert not self.pending[e], "pending instrs at barrier on " + e
        for e in ENGS:
            seen = self.seen[e]
            for k, v in self.val.items():
                if v > 0 and seen.get(k, 0) < v:
                    self.prog[e].append(('w', k, v))
                    seen[k] = v
        self.lastw.clear()
        self.readers.clear()

    def emit(self):
        sems = self.sems
        with self.nc.Block() as block:
            for e, meth in (('sp', block.sync), ('act', block.scalar), ('pe', block.tensor),
                            ('dve', block.vector), ('pool', block.gpsimd)):
                items = self.prog[e]

                def body(eng, items=items):
                    for it in items:
                        if it[0] == 'w':
                            eng.wait_ge(sems[it[1]], it[2])
                        else:
                            r = it[1](eng)
                            if it[2] is not None:
                                r.then_inc(sems[it[2]], it[3])
                meth(body)
        self.prog = {e: [] for e in ENGS}

    def next_bank(self):
        b = self.bank
        self.bank = (b + 1) % 8
        return b

    def mm(self, out, lhsT, rhs, start, stop, reads, writes, inc):
        self.ins('pe', lambda eng: eng.matmul(out, lhsT, rhs, start=start, stop=stop), reads, writes, inc)

    def transpose(self, out, in_, ident, reads, writes, inc):
        self.ins('pe', lambda eng: eng.transpose(out, in_, ident), reads, writes, inc)

    def act(self, out, in_, func, reads, writes, bias=0.0, scale=1.0, e='act'):
        self.ins('act', lambda eng: eng.activation(out, in_, func, bias=bias, scale=scale), reads, writes)

    def copy(self, e, out, in_, reads, writes):
        if e == 'act':
            self.ins('act', lambda eng: eng.activation(out, in_, AF.Copy), reads, writes)
        else:
            self.ins(e, lambda eng: eng.tensor_copy(out, in_), reads, writes)

    def tt(self, e, out, in0, in1, op, reads, writes):
        self.ins(e, lambda eng: eng.tensor_tensor(out, in0, in1, op), reads, writes)

    def ts(self, e, out, in0, s1, s2, op0, op1, reads, writes):
        if s2 is None:
            self.ins(e, lambda eng: eng.tensor_scalar(out, in0, s1, None, op0), reads, writes)
        else:
            self.ins(e, lambda eng: eng.tensor_scalar(out, in0, s1, s2, op0, op1), reads, writes)

    def stt(self, e, out, in0, scalar, in1, op0, op1, reads, writes):
        self.ins(e, lambda eng: eng.scalar_tensor_tensor(out, in0, scalar, in1, op0, op1), reads, writes)


_uid = [0]


def sb(nc, es, name, shape, dt):
    _uid[0] += 1
    return es.enter_context(nc.sbuf_tensor("%s_%d" % (name, _uid[0]), list(shape), dt))


def load_xT(P, nc, es, PS, x_in, XT, ident, T, D, tag="x"):
    NC = D // 128
    xs = [sb(nc, es, "%s_xs%d" % (tag, i), [128, D], BF16) for i in range(3)]
    cp = 0
    for i in range(T // 128):
        xb = xs[i % 3]
        kx = (tag + 'xs', i % 3)
        P.dma('pool', xb[:], x_in[i * 128:(i + 1) * 128, :], reads=[], writes=[kx])
        for g in range(NC // 8):
            b = P.next_bank()
            pb = PS[b][:].bitcast(BF16)
            for m in range(8):
                c = g * 8 + m
                P.transpose(pb[:, m * 128:(m + 1) * 128], xb[:, c * 128:(c + 1) * 128], ident,
                            reads=[kx], writes=[('ps', b)], inc=(m == 7))
            dst = XT[:, g * 8:(g + 1) * 8, i * 128:(i + 1) * 128]
            src = pb.rearrange("p (a b) -> p a b", a=8)
            P.copy('act' if cp % 2 == 0 else 'dve', dst, src, reads=[('ps', b)], writes=[('XT', i)])
            cp += 1


def layer_norm_pass(P, nc, es, Y, x_out, lng, lnb, T, D, tag="ln", XT=None, ident=None, PS=None):
    gb = sb(nc, es, tag + "_g", [128, D], F32)
    bb = sb(nc, es, tag + "_b", [128, D], F32)
    P.dma('sp', gb[:], lng.partition_broadcast(128), writes=[(tag, 'g')])
    P.dma('sp', bb[:], lnb.partition_broadcast(128), writes=[(tag, 'b')])
    NB = 4 if XT is None else 2
    if XT is not None:
        ybf = sb(nc, es, tag + "_ybf", [128, D], BF16)
    yb = [sb(nc, es, "%s_y%d" % (tag, i), [128, D], F32) for i in range(NB)]
    st = [sb(nc, es, "%s_st%d" % (tag, i), [128, D // 512, 6], F32) for i in range(NB)]
    mv = [sb(nc, es, "%s_mv%d" % (tag, i), [128, 2], F32) for i in range(NB)]
    rs = [sb(nc, es, "%s_rs%d" % (tag, i), [128, 2], F32) for i in range(NB)]
    for i in range(T // 128):
        s = i % NB
        y = yb[s]
        ky = (tag + 'y', s)
        P.dma('sp', y[:], Y[i * 128:(i + 1) * 128, :], writes=[ky])
        kst = (tag + 'st', s)
        for c in range(D // 512):
            P.ins('dve', lambda eng, o=st[s][:, c, :], a=y[:, c * 512:(c + 1) * 512]: eng.bn_stats(o, a),
                  reads=[ky], writes=[(tag + 'st', s, c)])
        P.ins('dve', lambda eng, o=mv[s][:], a=st[s][:]: eng.bn_aggr(o, a),
              reads=[(tag + 'st', s, c) for c in range(D // 512)], writes=[(tag + 'mv', s)])
        P.ts('dve', rs[s][:, 0:1], mv[s][:, 1:2], LN_EPS, None, ALU.add, None,
             reads=[(tag + 'mv', s)], writes=[(tag + 'rs', s, 0)])
        P.act(rs[s][:, 0:1], rs[s][:, 0:1], AF.Sqrt, reads=[(tag + 'rs', s, 0)], writes=[(tag + 'rs', s, 0)])
        P.ins('dve', lambda eng, o=rs[s][:, 0:1]: eng.reciprocal(o, o),
              reads=[(tag + 'rs', s, 0)], writes=[(tag + 'rs', s, 0)])
        P.stt('dve', rs[s][:, 1:2], mv[s][:, 0:1], -1.0, rs[s][:, 0:1], ALU.mult, ALU.mult,
              reads=[(tag + 'mv', s), (tag + 'rs', s, 0)], writes=[(tag + 'rs', s, 1)])
        P.act(y[:], y[:], AF.Identity, reads=[ky, (tag + 'rs', s, 0), (tag + 'rs', s, 1)], writes=[ky],
              bias=rs[s][:, 1:2], scale=rs[s][:, 0:1])
        P.tt('dve', y[:], y[:], gb[:], ALU.mult, reads=[ky, (tag, 'g')], writes=[ky, (ky, 'lo'), (ky, 'hi')])
        XS = D // 4
        P.tt('pool', y[:, XS:], y[:, XS:], bb[:, XS:], ALU.add, reads=[ky, (tag, 'b')], writes=[(ky, 'hi')])
        P.tt('dve', y[:, 0:XS], y[:, 0:XS], bb[:, 0:XS], ALU.add, reads=[ky, (tag, 'b')], writes=[(ky, 'lo')])
        P.dma('pool', x_out[i * 128:(i + 1) * 128, :], y[:], reads=[ky, (ky, 'lo'), (ky, 'hi')], writes=[])
        if XT is not None:
            P.copy('act', ybf[:], y[:], reads=[ky, (ky, 'lo'), (ky, 'hi')], writes=[tag + 'ybf'])
            for g in range(D // 1024):
                b = P.next_bank()
                pb = PS[b][:].bitcast(BF16)
                for m in range(8):
                    c = g * 8 + m
                    P.transpose(pb[:, m * 128:(m + 1) * 128], ybf[:, c * 128:(c + 1) * 128], ident,
                                reads=[tag + 'ybf'], writes=[('ps', b)], inc=(m == 7))
                P.copy('act', XT[:, g * 8:(g + 1) * 8, i * 128:(i + 1) * 128],
                       pb.rearrange("p (a b) -> p a b", a=8), reads=[('ps', b)], writes=[('XT', i)])


def layer_norm_tiles(P, nc, es, PS, Y, STd, x_out, lng, lnb, T, D, tag, XT=None, ident=None):
    NI = T // 128
    NB = D // 512
    CW = 1024
    NCB = D // CW
    gb = sb(nc, es, tag + "_g", [128, D], F32)
    bb = sb(nc, es, tag + "_b", [128, D], F32)
    P.dma('sp', gb[:], lng.partition_broadcast(128), writes=[(tag, 'g')])
    P.dma('sp', bb[:], lnb.partition_broadcast(128), writes=[(tag, 'b')])
    ST = sb(nc, es, tag + "_ST", [128, NI, NB, 6], F32)
    P.dma('sp', ST[:].rearrange("p a b c -> p (a b c)"), STd, writes=[tag + 'ST'])
    MV = sb(nc, es, tag + "_MV", [128, NI, 2], F32)
    RS = sb(nc, es, tag + "_RS", [128, NI, 2], F32)
    for i in range(NI):
        P.ins('dve', lambda eng, i=i: eng.bn_aggr(MV[:, i, :], ST[:, i, :, :]), reads=[tag + 'ST'], writes=[(tag + 'MV', i)])
    mvk = [(tag + 'MV', i) for i in range(NI)]
    P.ts('dve', RS[:, :, 0:1], MV[:, :, 1:2], LN_EPS, None, ALU.add, None, reads=mvk, writes=[tag + 'RS0'])
    P.act(RS[:, :, 0:1], RS[:, :, 0:1], AF.Sqrt, reads=[tag + 'RS0'], writes=[tag + 'RS0'])
    P.ins('dve', lambda eng: eng.reciprocal(RS[:, :, 0:1], RS[:, :, 0:1]), reads=[tag + 'RS0'], writes=[tag + 'RS0'])
    P.stt('dve', RS[:, :, 1:2], MV[:, :, 0:1], -1.0, RS[:, :, 0:1], ALU.mult, ALU.mult,
          reads=mvk + [tag + 'RS0'], writes=[tag + 'RS1'])
    NYB = 6
    yt = Rot(tag + 'yt', [sb(nc, es, "%s_yt%d" % (tag, i), [128, CW], F32) for i in range(NYB)])
    if XT is not None:
        ybf = Rot(tag + 'ybf', [sb(nc, es, "%s_ybf%d" % (tag, i), [128, CW], BF16) for i in range(3)])
    tiles = [(i, cb) for i in range(NI) for cb in range(NCB)]
    NTL = len(tiles)
    stA = {}
    stB = {}

    def stageA(n):
        i, cb = tiles[n]
        csl = slice(cb * CW, (cb + 1) * CW)
        y, ky = yt.next()
        P.dma('sp', y[:], Y[i * 128:(i + 1) * 128, csl], writes=[ky])
        P.act(y[:], y[:], AF.Identity, reads=[ky, tag + 'RS0', tag + 'RS1'], writes=[ky],
              bias=RS[:, i, 1:2], scale=RS[:, i, 0:1])
        P.tt('dve', y[:], y[:], gb[:, csl], ALU.mult, reads=[ky, (tag, 'g')], writes=[ky])
        P.tt('pool' if n % 2 == 0 else 'dve', y[:], y[:], bb[:, csl], ALU.add, reads=[ky, (tag, 'b')], writes=[ky])
        P.dma('pool', x_out[i * 128:(i + 1) * 128, csl], y[:], reads=[ky], writes=[])
        stA[n] = (y, ky)

    def stageB(n):
        y, ky = stA.pop(n)
        yb, kyb = ybf.next()
        P.copy('act', yb[:], y[:], reads=[ky], writes=[kyb])
        b = P.next_bank()
        pb = PS[b][:].bitcast(BF16)
        for m in range(8):
            P.transpose(pb[:, m * 128:(m + 1) * 128], yb[:, m * 128:(m + 1) * 128], ident,
                        reads=[kyb], writes=[('ps', b)], inc=(m == 7))
        stB[n] = (b, pb)

    def stageC(n):
        i, cb = tiles[n]
        b, pb = stB.pop(n)
        P.copy('act' if n % 2 == 0 else 'dve', XT[:, cb * 8:(cb + 1) * 8, i * 128:(i + 1) * 128],
               pb.rearrange("p (a b) -> p a b", a=8), reads=[('ps', b)], writes=[('XT', i, cb)])
    for n in range(NTL + 2):
        if n < NTL:
            stageA(n)
        if XT is not None:
            if 0 <= n - 1 < NTL:
                stageB(n - 1)
            if 0 <= n - 2 < NTL:
                stageC(n - 2)


def final_ln_phase(P, nc, PS, ident_d, Y, x_out, lng, lnb, T, D, tag, want_xt, STd=None):
    if not want_xt:
        with ExitStack() as es:
            if STd is not None:
                layer_norm_tiles(P, nc, es, PS, Y, STd, x_out, lng, lnb, T, D, tag)
            else:
                layer_norm_pass(P, nc, es, Y, x_out, lng, lnb, T, D, tag=tag)
            P.barrier()
            P.emit()
        return None
    esX = ExitStack()
    XT = sb(nc, esX, tag + "_XTn", [128, D // 128, T], BF16)
    with ExitStack() as es:
        ident = sb(nc, es, tag + "_ident", [128, 128], BF16)
        P.dma('pool', ident[:], ident_d, writes=[tag + 'ident'])
        if STd is not None:
            layer_norm_tiles(P, nc, es, PS, Y, STd, x_out, lng, lnb, T, D, tag, XT=XT, ident=ident[:])
        else:
            layer_norm_pass(P, nc, es, Y, x_out, lng, lnb, T, D, tag=tag, XT=XT, ident=ident[:], PS=PS)
        P.barrier()
        P.emit()
    return (esX, XT)


def stage_ffn(P, nc, PS, ident_d, x_in, x_out, w_up, convp, w_down, lng, lnb, G, Y, T, D, FC, xt_in=None, STd=None):
    NC = D // 128
    NT = T // 512
    F = FC * 128
    wup_v = w_up.rearrange("(c p) n -> p c n", p=128)
    wdn_v = w_down.rearrange("(c p) n -> p c n", p=128)
    with ExitStack() as es:
        if xt_in is None:
            ident = sb(nc, es, "f_ident", [128, 128], BF16)
            P.dma('pool', ident[:], ident_d, writes=['ident'])
            XT = sb(nc, es, "f_XT", [128, NC, T], BF16)
            with ExitStack() as es0:
                load_xT(P, nc, es0, PS, x_in, XT, ident[:], T, D, tag="f")
                P.barrier()
                P.emit()
        else:
            XT = xt_in[1]
        cw = sb(nc, es, "f_cw", [128, 4, 2 * FC], F32)
        P.dma('sp', cw[:], convp.rearrange("p (k c) -> p k c", k=4), writes=['cw'])
        NWB = 2
        wg = [sb(nc, es, "f_wg%d" % i, [128, NC, 128], BF16) for i in range(NWB)]
        wv = [sb(nc, es, "f_wv%d" % i, [128, NC, 128], BF16) for i in range(NWB)]
        hb = [[sb(nc, es, "f_h%d_%d" % (i, s), [128, 514], F32) for s in range(2)] for i in range(2)]
        ab = [[sb(nc, es, "f_a%d_%d" % (i, s), [128, 512], F32) for s in range(2)] for i in range(2)]
        sg = [sb(nc, es, "f_sg%d" % s, [128, 512], F32) for s in range(2)]
        go = [sb(nc, es, "f_go%d" % i, [128, T], BF16) for i in range(2)]
        xt_keys = [('XT', i) for i in range(T // 128)]
        it = 0
        def load_w(j):
            wb = j % NWB
            P.dma('pool', wg[wb][:], wup_v[:, :, j * 128:(j + 1) * 128], writes=[('wg', wb)])
            P.dma('pool', wv[wb][:], wup_v[:, :, F + j * 128:F + (j + 1) * 128], writes=[('wv', wb)])
        load_w(0)
        for j in range(FC):
            wb = j % NWB
            if j + 1 < FC:
                load_w(j + 1)
            gob = go[j % 2]
            kgo = ('go', j % 2)
            for tt in range(NT):
                s = it % 2
                it += 1
                banks = []
                for (w, kw) in ((wg[wb], ('wg', wb)), (wv[wb], ('wv', wb))):
                    b = P.next_bank()
                    banks.append(b)
                    for c in range(NC):
                        P.mm(PS[b][:], w[:, c, :], XT[:, c, tt * 512:(tt + 1) * 512], c == 0, c == NC - 1,
                             reads=[kw], writes=[('ps', b)], inc=(c == NC - 1))
                for gv in range(2):
                    h = hb[gv][s]
                    a = ab[gv][s]
                    kh = ('h', gv, s)
                    ka = ('a', gv, s)
                    col = gv * FC + j
                    if tt == 0:
                        P.ins('dve', lambda eng, o=h[:, 0:2]: eng.memset(o, 0.0), writes=[(kh, 'halo')])
                    else:
                        hp = hb[gv][1 - s]
                        P.copy('act', h[:, 0:2], hp[:, 512:514], reads=[('h', gv, 1 - s)], writes=[(kh, 'halo')])
                    P.copy('act', h[:, 2:514], PS[banks[gv]][:], reads=[('ps', banks[gv])], writes=[kh])
                    P.ts('dve', a[:], h[:, 2:514], cw[:, 2, col:col + 1], cw[:, 3, col:col + 1], ALU.mult, ALU.add,
                         reads=[kh, 'cw'], writes=[ka])
                    P.stt('dve', a[:], h[:, 1:513], cw[:, 1, col:col + 1], a[:], ALU.mult, ALU.add,
                          reads=[kh, (kh, 'halo'), ka, 'cw'], writes=[ka])
                    P.stt('dve', a[:], h[:, 0:512], cw[:, 0, col:col + 1], a[:], ALU.mult, ALU.add,
                          reads=[kh, (kh, 'halo'), ka, 'cw'], writes=[ka])
                P.act(sg[s][:], ab[0][s][:], AF.Silu, reads=[('a', 0, s)], writes=[('sg', s)])
                P.tt('dve', gob[:, tt * 512:(tt + 1) * 512], sg[s][:], ab[1][s][:], ALU.mult,
                     reads=[('sg', s), ('a', 1, s)], writes=[kgo])
            P.dma('sp', G[:, :, j, :].rearrange("i p t -> p i t"), gob[:].rearrange("p (i t) -> p i t", t=128),
                  reads=[kgo], writes=['G'])
        P.barrier()
        P.emit()
    if xt_in is not None:
        xt_in[0].close()
    DB = 512
    with ExitStack() as es:
        wd = sb(nc, es, "f_wd", [128, FC, DB], BF16)
        gt = [sb(nc, es, "f_gt%d" % i, [128, FC, 128], BF16) for i in range(2)]
        xr = [sb(nc, es, "f_xr%d" % i, [128, DB], F32) for i in range(2)]
        yo = [sb(nc, es, "f_yo%d" % i, [128, DB], F32) for i in range(2)]
        if STd is not None:
            ST = sb(nc, es, "f_ST", [128, T // 128, D // 512, 6], F32)
        NSPL = 8
        bounds = [(FC * q) // NSPL for q in range(NSPL + 1)]
        ends = set(bounds[1:])
        it = 0
        for db in range(D // DB):
            for q in range(NSPL):
                lo, hi = bounds[q], bounds[q + 1]
                if hi <= lo:
                    continue
                P.dma('pool', wd[:, lo:hi, :], wdn_v[:, lo:hi, db * DB:(db + 1) * DB], writes=[('wd', q)])
            for i in range(T // 128):
                s = it % 2
                it += 1
                P.dma('sp', gt[s][:], G[i], reads=['G'], writes=[('gt', s)])
                P.dma('sp', xr[s][:], x_in[i * 128:(i + 1) * 128, db * DB:(db + 1) * DB], writes=[('xr', s)])
                b = P.next_bank()
                for k in range(FC):
                    q = 0
                    while k >= bounds[q + 1]:
                        q += 1
                    P.mm(PS[b][:], gt[s][:, k, :], wd[:, k, :], k == 0, k == FC - 1,
                         reads=[('gt', s), ('wd', q)], writes=[('ps', b)], inc=((k + 1) in ends))
                P.stt('dve', yo[s][:], xr[s][:], ALPHA, PS[b][:], ALU.mult, ALU.add,
                      reads=[('xr', s), ('ps', b)], writes=[('yo', s)])
                if STd is not None:
                    P.ins('dve', lambda eng, o=ST[:, i, db, :], a=yo[s][:]: eng.bn_stats(o, a),
                          reads=[('yo', s)], writes=[('ST', i, db)])
                P.dma('act', Y[i * 128:(i + 1) * 128, db * DB:(db + 1) * DB], yo[s][:], reads=[('yo', s)], writes=['Y'])
        if STd is not None:
            P.dma('sp', STd, ST[:].rearrange("p a b c -> p (a b c)"),
                  reads=[('ST', i, db) for i in range(T // 128) for db in range(D // DB)], writes=['STd'])
        P.barrier()
        P.emit()
    with ExitStack() as es:
        if STd is not None:
            layer_norm_tiles(P, nc, es, PS, Y, STd, x_out, lng, lnb, T, D, "fl")
        else:
            layer_norm_pass(P, nc, es, Y, x_out, lng, lnb, T, D, tag="fl")
        P.barrier()
        P.emit()


POOL_WINDOWS = (2, 4, 8, 16)


def pool_consts():
    out = np.zeros((128, 12, 128), np.float32)
    s = np.arange(128)[:, None]
    t = np.arange(128)[None, :]
    for g, w in enumerate(POOL_WINDOWS):
        inwin = ((t - s) >= 0) & ((t - s) < w)
        out[:, 3 * g + 0, :] = inwin / float(w) - (s == t)
        pw = ((t - (s - 128)) >= 0) & ((t - (s - 128)) < w)
        out[:, 3 * g + 1, :] = pw / float(w)
        cnt = np.minimum(t + 1, w).astype(np.float32)
        out[:, 3 * g + 2, :] = inwin / cnt - (s == t)
    return out


def stage_pool(P, nc, PS, x_in, x_out, pool_w, pool_scale, poolc_d, lng, lnb, Y, T, D, ident_d=None, want_xt=False,
               STd=None):
    NC = D // 128
    NG = 4
    CG = NC // NG
    GD = D // NG
    NI = T // 128
    with ExitStack() as es:
        DT = sb(nc, es, "p_DT", [128, NC, T], BF16)
        with ExitStack() as es0:
            pc = sb(nc, es0, "p_pc", [128, 12, 128], F32)
            P.dma('sp', pc[:], poolc_d, writes=['pc'])
            xs = [sb(nc, es0, "p_xs%d" % i, [128, D], F32) for i in range(3)]
            cp = 0
            for i in range(NI):
                xb = xs[i % 3]
                P.dma('sp', xb[:], x_in[i * 128:(i + 1) * 128, :], writes=[('pxs', i % 3)])
                for c4 in range(NC // 4):
                    b = P.next_bank()
                    for m in range(4):
                        c = c4 * 4 + m
                        g = c // CG
                        o = PS[b][:, m * 128:(m + 1) * 128]
                        bm = pc[:, 3 * g + (2 if i == 0 else 0), :]
                        last = (i == 0)
                        P.mm(o, xb[:, c * 128:(c + 1) * 128], bm, True, last,
                             reads=[('pxs', i % 3), 'pc'], writes=[('ps', b)], inc=(last and m == 3))
                        if i > 0:
                            xp = xs[(i - 1) % 3]
                            P.mm(o, xp[:, c * 128:(c + 1) * 128], pc[:, 3 * g + 1, :], False, True,
                                 reads=[('pxs', (i - 1) % 3), 'pc'], writes=[('ps', b)], inc=(m == 3))
                    dst = DT[:, c4 * 4:(c4 + 1) * 4, i * 128:(i + 1) * 128]
                    src = PS[b][:].rearrange("p (a b) -> p a b", a=4)
                    P.copy('act' if cp % 2 == 0 else 'dve', dst, src, reads=[('ps', b)], writes=[('DT', i)])
                    cp += 1
            P.barrier()
            P.emit()
        psc = sb(nc, es, "p_psc", [128, D], F32)
        P.dma('sp', psc[:], pool_scale.partition_broadcast(128), writes=['psc'])
        wp = [sb(nc, es, "p_wp%d" % i, [128, CG, GD], BF16) for i in range(2)]
        xr = [sb(nc, es, "p_xr%d" % i, [128, 512], F32) for i in range(2)]
        t1 = [sb(nc, es, "p_t1%d" % i, [128, 512], F32) for i in range(2)]
        yo = [sb(nc, es, "p_yo%d" % i, [128, 512], F32) for i in range(2)]
        if STd is not None:
            ST = sb(nc, es, "p_ST", [128, NI, D // 512, 6], F32)
        it = 0
        for g in range(NG):
            w = wp[g % 2]
            P.dma('pool', w[:], pool_w[g].rearrange("(c p) e -> p c e", p=128), writes=[('wp', g % 2)])
            for i in range(NI):
                for eb in range(GD // 512):
                    s = it % 2
                    it += 1
                    c0 = g * GD + eb * 512
                    P.dma('act', xr[s][:], x_in[i * 128:(i + 1) * 128, c0:c0 + 512], writes=[('xr', s)])
                    b = P.next_bank()
                    for c in range(CG):
                        P.mm(PS[b][:], DT[:, g * CG + c, i * 128:(i + 1) * 128], w[:, c, eb * 512:(eb + 1) * 512],
                             c == 0, c == CG - 1, reads=[('wp', g % 2)], writes=[('ps', b)], inc=(c == CG - 1))
                    P.tt('dve', t1[s][:], PS[b][:], psc[:, c0:c0 + 512], ALU.mult,
                         reads=[('ps', b), 'psc'], writes=[('t1', s)])
                    P.stt('dve', yo[s][:], xr[s][:], ALPHA, t1[s][:], ALU.mult, ALU.add,
                          reads=[('xr', s), ('t1', s)], writes=[('yo', s)])
                    if STd is not None:
                        P.ins('dve', lambda eng, o=ST[:, i, c0 // 512, :], a=yo[s][:]: eng.bn_stats(o, a),
                              reads=[('yo', s)], writes=[('ST', i, c0 // 512)])
                    P.dma('sp', Y[i * 128:(i + 1) * 128, c0:c0 + 512], yo[s][:], reads=[('yo', s)], writes=['Y'])
        if STd is not None:
            P.dma('sp', STd, ST[:].rearrange("p a b c -> p (a b c)"),
                  reads=[('ST', i, cb) for i in range(NI) for cb in range(D // 512)], writes=['STd'])
        P.barrier()
        P.emit()
    return final_ln_phase(P, nc, PS, ident_d, Y, x_out, lng, lnb, T, D, "pl", want_xt, STd=STd)


HD = 128
SEG = dict(qa=0, kc=2048, vc=2560, ks=3072, vs=3584, kw=4096, vw=4608, gt=5120, qb=5168, kb=7216, vb=9264)
IN_DIM = 11312
SCALE = float(128 ** -0.5)
BIG = 30000.0
TSEQ = 2048
NCMP = 127


def attn_consts():
    T = TSEQ
    c = {}
    c['ident'] = np.eye(128, dtype=np.float32)
    rot = np.zeros((128, 128), np.float32)
    for dp in range(64):
        rot[dp + 64, dp] = -1.0
    for dp in range(64, 128):
        rot[dp - 64, dp] = 1.0
    c['rotm'] = rot
    inv_freq = (1.0 / (10000.0 ** (np.arange(0, 128, 2, dtype=np.float32) / 128.0))).astype(np.float32)

    def cs(pos):
        ang = pos.astype(np.float32)[:, None] * inv_freq[None, :]
        co = np.cos(ang).astype(np.float32).T
        si = np.sin(ang).astype(np.float32).T
        return np.concatenate([co, co], 0), np.concatenate([si, si], 0)
    c['cos'], c['sin'] = cs(np.arange(T))
    cmp_end = np.arange(NCMP) * 16 + 31
    c['cosc'], c['sinc'] = cs(cmp_end)
    t = np.arange(T)
    nv = np.where(cmp_end[:, None] <= t[None, :], 0.0, -BIG).astype(np.float32)
    c['negvalid'] = np.concatenate([nv, np.zeros((1, T), np.float32)], 0)
    c_start = np.arange(NCMP)[:, None] * 16
    b_start = np.arange(32)[None, :] * 64
    ov = np.clip(np.minimum(c_start + 32, b_start + 64) - np.maximum(c_start, b_start), 0, None) / 32.0
    vx = np.zeros((128, 33), np.float32)
    vx[:NCMP, 0] = 1.0
    vx[:NCMP, 1:] = ov
    c['vcx'] = vx
    blk = np.arange(32)[None, :]
    cur = (t // 64)[:, None]
    forced = (blk == 0) | (blk == cur) | (blk == cur - 1)
    valid = blk * 64 <= t[:, None]
    vnf = (valid & ~forced).astype(np.float32)
    addc = np.where(forced, 1e9, np.where(valid, 0.0, -1e30)).astype(np.float32)
    c['vnf'] = vnf.reshape(16, 128, 32).transpose(1, 0, 2).copy()
    c['addc'] = addc.reshape(16, 128, 32).transpose(1, 0, 2).copy()
    em = np.zeros((128, 16, 128), np.float32)
    for a in range(16):
        for s in range(128):
            em[(a * 128 + s) // 64, a, s] = 1.0
    c['expm'] = em
    sl = np.arange(128)[:, None]
    tl = np.arange(512)[None, :]
    negw = np.zeros((128, 8, 512), np.float32)
    for r in range(-4, 4):
        d = tl - 128 * r - sl
        negw[:, r + 4, :] = np.where((d >= 0) & (d < 512), 0.0, -BIG)
    c['negw'] = negw
    negsb = np.zeros((128, 4, 512), np.float32)
    for r in range(4):
        d = tl - 128 * r - sl
        negsb[:, r, :] = np.where(d > 0, 0.0, -BIG)
    c['negsb'] = negsb
    j = np.arange(128)[:, None]
    s = np.arange(128)[None, :]
    c['tri'] = (j > s).astype(np.float32)
    c['onesmtri'] = (j <= s).astype(np.float32)
    return c


class ConstBlob:
    def __init__(self, consts):
        self.off = {}
        cols = []
        o = 0
        for k, v in consts.items():
            v2 = v.reshape(128, -1)
            self.off[k] = (o, v2.shape[1], v.shape)
            cols.append(v2)
            o += v2.shape[1]
        self.arr = np.ascontiguousarray(np.concatenate(cols, 1).astype(np.float32))
        self.ncols = o

    def load(self, P, nc, es, blob_d, name, dt, q='pool'):
        o, n, shape = self.off[name]
        t = sb(nc, es, "c_" + name, list(shape), dt)
        src = blob_d[:, o:o + n]
        dst = t[:]
        if len(shape) == 3:
            src = src.rearrange("p (a b) -> p a b", a=shape[1])
        P.dma(q, dst, src, writes=[('const', name)])
        return t


class Rot:
    def __init__(self, name, bufs):
        self.name = name
        self.bufs = bufs
        self.i = 0

    def next(self):
        i = self.i
        self.i = (i + 1) % len(self.bufs)
        return self.bufs[i], (self.name, i)


def stage_attn(P, nc, PS, CB, blob_d, x_in, x_out, w_in, w_out, cmp_wk, cmp_wv, pekT, pevT, lng, lnb, S, T, D, stop=99,
               want_xt=False, STd=None):
    NC = D // 128
    NI = T // 128
    NT = T // 512
    win_v = w_in.rearrange("(c p) n -> p c n", p=128)
    QA, KCR, VCR, KS, KW, QB, KB = S['QA'], S['KCR'], S['VCR'], S['KS'], S['KW'], S['QB'], S['KB']
    VS, VW, VB, GTMd, Y = S['VS'], S['VW'], S['VB'], S['GTM'], S['Y']
    with ExitStack() as es:
        XT = sb(nc, es, "a_XT", [128, NC, T], BF16)
        ident = CB.load(P, nc, es, blob_d, 'ident', BF16)
        rotm = CB.load(P, nc, es, blob_d, 'rotm', BF16)
        cosT = CB.load(P, nc, es, blob_d, 'cos', F32, q='sp')
        sinT = CB.load(P, nc, es, blob_d, 'sin', F32, q='sp')
        with ExitStack() as es0:
            load_xT(P, nc, es0, PS, x_in, XT, ident[:], T, D, tag="a")
            P.barrier()
            P.emit()
        wb = Rot('wb', [sb(nc, es, "a_wb%d" % i, [128, NC, 256], BF16) for i in range(2)])
        ost = Rot('ost', [sb(nc, es, "a_ost%d" % i, [128, T], BF16) for i in range(2)])
        qraw = Rot('qraw', [sb(nc, es, "a_qr%d" % i, [128, 512], BF16) for i in range(2)])
        t1r = Rot('t1', [sb(nc, es, "a_t1%d" % i, [128, 512], F32) for i in range(1)])
        t2r = Rot('t2', [sb(nc, es, "a_t2%d" % i, [128, 512], F32) for i in range(1)])
        vst = Rot('vst', [sb(nc, es, "a_vst%d" % i, [128, NI, 256], BF16) for i in range(1)])
        gtm = sb(nc, es, "a_gtm", [128, NI, 48], F32)
        cpi = [0]
        fm = [('qa', 16, True, QA), ('kc', 4, False, KCR), ('vc', 4, False, VCR), ('ks', 4, True, KS),
              ('kw', 4, True, KW), ('qb', 16, False, QB), ('kb', 16, False, KB)]
        for (sname, nch, rope, dest) in fm:
            for ch in range(nch):
                if ch % 2 == 0:
                    col = SEG[sname] + ch * 128
                    w, kw_ = wb.next()
                    P.dma('pool', w[:], win_v[:, :, col:col + 256], writes=[kw_])
                wsl = slice((ch % 2) * 128, (ch % 2) * 128 + 128)
                o, ko = ost.next()
                for tt in range(NT):
                    b = P.next_bank()
                    tsl = slice(tt * 512, (tt + 1) * 512)
                    for c in range(NC):
                        P.mm(PS[b][:], w[:, c, wsl], XT[:, c, tsl], c == 0, c == NC - 1,
                             reads=[kw_], writes=[('ps', b)], inc=(c == NC - 1))
                    if rope:
                        qr, kq = qraw.next()
                        P.copy('act', qr[:], PS[b][:], reads=[('ps', b)], writes=[kq])
                        b2 = P.next_bank()
                        P.mm(PS[b2][:], rotm[:], qr[:], True, True, reads=[kq, ('const', 'rotm')],
                             writes=[('ps', b2)], inc=True)
                        t1, k1 = t1r.next()
                        t2, k2 = t2r.next()
                        P.tt('dve', t1[:], PS[b][:], cosT[:, tsl], ALU.mult, reads=[('ps', b), ('const', 'cos')],
                             writes=[k1])
                        P.tt('dve', t2[:], PS[b2][:], sinT[:, tsl], ALU.mult, reads=[('ps', b2), ('const', 'sin')],
                             writes=[k2])
                        P.tt('pool', o[:, tsl], t1[:], t2[:], ALU.add, reads=[k1, k2], writes=[ko])
                    else:
                        P.copy('act' if cpi[0] % 2 == 0 else 'dve', o[:, tsl], PS[b][:], reads=[('ps', b)], writes=[ko])
                        cpi[0] += 1
                P.dma('sp', dest[ch], o[:], reads=[ko], writes=[('dram', sname)])
        w, kw_ = wb.next()
        P.dma('pool', w[:, :, 0:48], win_v[:, :, SEG['gt']:SEG['gt'] + 48], writes=[kw_])
        for i in range(NI):
            b = P.next_bank()
            for c in range(NC):
                P.mm(PS[b][:, 0:48], XT[:, c, i * 128:(i + 1) * 128], w[:, c, 0:48], c == 0, c == NC - 1,
                     reads=[kw_], writes=[('ps', b)], inc=(c == NC - 1))
            P.act(gtm[:, i, :], PS[b][:, 0:48], AF.Sigmoid, reads=[('ps', b)], writes=['gtm'])
        P.dma('sp', GTMd, gtm[:].rearrange("p a b -> p (a b)"), reads=['gtm'], writes=[('dram', 'gtm')])
        for (sname, ncol, dest) in (('vs', 512, VS), ('vw', 512, VW), ('vb', 2048, VB)):
            for sl_ in range(ncol // 256):
                col = SEG[sname] + sl_ * 256
                w, kw_ = wb.next()
                P.dma('pool', w[:], win_v[:, :, col:col + 256], writes=[kw_])
                vt, kv = vst.next()
                for i2 in range(NI // 2):
                    b = P.next_bank()
                    for m in range(2):
                        i = i2 * 2 + m
                        for c in range(NC):
                            P.mm(PS[b][:, m * 256:(m + 1) * 256], XT[:, c, i * 128:(i + 1) * 128], w[:, c, :],
                                 c == 0, c == NC - 1, reads=[kw_], writes=[('ps', b)],
                                 inc=(c == NC - 1 and m == 1))
                    P.copy('act' if cpi[0] % 2 == 0 else 'dve', vt[:, i2 * 2:(i2 + 1) * 2, :],
                           PS[b][:].rearrange("p (a b) -> p a b", a=2), reads=[('ps', b)], writes=[kv])
                    cpi[0] += 1
                P.dma('sp', dest[:, sl_ * 256:(sl_ + 1) * 256].rearrange("(a p) d -> p a d", p=128), vt[:],
                      reads=[kv], writes=[('dram', sname)])
        P.barrier()
        P.emit()
    if stop <= 1:
        return

    with ExitStack() as esO:
        OTA = sb(nc, esO, "a_OTA", [128, 16, T], BF16)
        esN = ExitStack()
        with esN as es:
            ident = CB.load(P, nc, es, blob_d, 'ident', BF16)
            KCT = sb(nc, es, "a_KCT", [128, 4, 128], BF16)
            VCX = sb(nc, es, "a_VCX", [128, 4, 161], BF16)
            GTM = sb(nc, es, "a_GTM", [128, NI, 48], F32)
            P.dma('sp', GTM[:].rearrange("p a b -> p (a b)"), GTMd, writes=['GTM'])
            with ExitStack() as esc:
                rotm = CB.load(P, nc, esc, blob_d, 'rotm', BF16)
                cosc = CB.load(P, nc, esc, blob_d, 'cosc', F32, q='sp')
                sinc = CB.load(P, nc, esc, blob_d, 'sinc', F32, q='sp')
                wk = sb(nc, esc, "a_wk", [128, 32, 128], BF16)
                wv = sb(nc, esc, "a_wv", [128, 32, 128], BF16)
                P.dma('pool', wk[:], cmp_wk.rearrange("l d e -> d l e"), writes=['wk'])
                P.dma('pool', wv[:], cmp_wv.rearrange("l d e -> d l e"), writes=['wv'])
                pek = sb(nc, esc, "a_pek", [128, 32], F32)
                pev = sb(nc, esc, "a_pev", [128, 32], F32)
                P.dma('sp', pek[:], pekT, writes=['pek'])
                P.dma('sp', pev[:], pevT, writes=['pev'])
                krr = Rot('krr', [sb(nc, esc, "a_kr%d" % i, [128, T], BF16) for i in range(2)])
                vrr = Rot('vrr', [sb(nc, esc, "a_vr%d" % i, [128, T], BF16) for i in range(2)])
                blk = Rot('blk', [sb(nc, esc, "a_blk%d" % i, [128, 128], BF16) for i in range(4)])
                craw = sb(nc, esc, "a_craw", [128, 128], BF16)
                ct1 = sb(nc, esc, "a_ct1", [128, 128], F32)
                ct2 = sb(nc, esc, "a_ct2", [128, 128], F32)
                o_, n_, _sh = CB.off['vcx']
                for k in range(4):
                    P.dma('pool', VCX[:, k, 128:161], blob_d[:, o_:o_ + n_], writes=[('VCXc', k)])
                for k in range(4):
                    kr, kkr = krr.next()
                    vr, kvr = vrr.next()
                    P.dma('sp', kr[:], KCR[k], writes=[kkr])
                    P.dma('sp', vr[:], VCR[k], writes=[kvr])
                    bK = 0
                    bV = 1
                    for (src, ksrc, pe_, kpe, isk) in ((kr, kkr, pek, 'pek', True), (vr, kvr, pev, 'pev', False)):
                        v3 = src[:].rearrange("p (n s) -> p n s", s=16)
                        for l in range(32):
                            bl, kbl = blk.next()
                            view = v3[:, (l // 16):(l // 16) + NCMP, l % 16]
                            P.ts('dve', bl[:, 0:NCMP], view, pe_[:, l:l + 1], None, ALU.add, None,
                                 reads=[ksrc, kpe], writes=[kbl])
                            if isk:
                                P.mm(PS[bK][:, 0:NCMP], wk[:, l, :], bl[:, 0:NCMP], l == 0, l == 31,
                                     reads=['wk', kbl], writes=[('ps', bK)], inc=True)
                            else:
                                P.mm(PS[bV][0:NCMP, 0:128], bl[:, 0:NCMP], wv[:, l, :], l == 0, l == 31,
                                     reads=['wv', kbl], writes=[('ps', bV)], inc=True)
                    P.copy('act', craw[:, 0:NCMP], PS[bK][:, 0:NCMP], reads=[('ps', bK)], writes=['craw'])
                    b2 = 2
                    P.mm(PS[b2][:, 0:NCMP], rotm[:], craw[:, 0:NCMP], True, True,
                         reads=['craw', ('const', 'rotm')], writes=[('ps', b2)], inc=True)
                    P.tt('dve', ct1[:, 0:NCMP], PS[bK][:, 0:NCMP], cosc[:], ALU.mult,
                         reads=[('ps', bK), ('const', 'cosc')], writes=['ct1'])
                    P.tt('dve', ct2[:, 0:NCMP], PS[b2][:, 0:NCMP], sinc[:], ALU.mult,
                         reads=[('ps', b2), ('const', 'sinc')], writes=['ct2'])
                    P.tt('dve', KCT[:, k, 0:NCMP], ct1[:, 0:NCMP], ct2[:, 0:NCMP], ALU.add,
                         reads=['ct1', 'ct2'], writes=[('KCT', k)])
                    P.copy('act', VCX[0:NCMP, k, 0:128], PS[bV][0:NCMP, 0:128], reads=[('ps', bV)],
                           writes=[('VCX', k)])
                P.barrier()
                P.emit()
            with ExitStack() as esn:
                if stop <= 2:
                    return
                identf = CB.load(P, nc, esn, blob_d, 'ident', F32, q='sp')
                negw = CB.load(P, nc, esn, blob_d, 'negw', BF16)
                negvalid = CB.load(P, nc, esn, blob_d, 'negvalid', BF16)
                expm = CB.load(P, nc, esn, blob_d, 'expm', BF16)
                vnf = CB.load(P, nc, esn, blob_d, 'vnf', F32, q='sp')
                addc = CB.load(P, nc, esn, blob_d, 'addc', F32, q='sp')
                QAg = [sb(nc, esn, "a_qa%d" % g, [128, T], BF16) for g in range(4)]
                KSk = sb(nc, esn, "a_ksk", [128, T], BF16)
                KWk = sb(nc, esn, "a_kwk", [128, T], BF16)
                VSk = sb(nc, esn, "a_vsk", [128, NI, 129], BF16)
                VWk = sb(nc, esn, "a_vwk", [128, NI, 129], BF16)
                OA = [sb(nc, esn, "a_oa%d" % g, [128, NI, 128], F32) for g in range(4)]
                IMP = sb(nc, esn, "a_imp", [128, NI, 32], F32)
                IMF = sb(nc, esn, "a_imf", [128, NI, 32], F32)
                IM2 = sb(nc, esn, "a_im2", [128, NI, 32], F32)
                M8 = sb(nc, esn, "a_m8", [128, NI, 8], F32)
                M8b = sb(nc, esn, "a_m8b", [128, NI, 8], F32)
                NSL = sb(nc, esn, "a_nsl", [128, NI, 32], BF16)
                NEGSEL = sb(nc, esn, "a_negsel", [32, T], BF16)
                CH = []
                for c in range(2):
                    CH.append(dict(
                        c=c, sbk=[4 * c, 4 * c + 1], ob=(4 * c + 2, 4 * c + 3), si=[0],
                        ET=Rot('et%d' % c, [sb(nc, esn, "a_et%d_%d" % (c, i), [128, 512], BF16) for i in range(3)]),
                        ORW=Rot('orw%d' % c, [sb(nc, esn, "a_orw%d_%d" % (c, i), [128, 4, 161], F32) for i in range(2)]),
                        RZ=sb(nc, esn, "a_rz%d" % c, [128, 4, 1], F32), SC=sb(nc, esn, "a_sc%d" % c, [128, 4, 1], F32),
                        TMPO=sb(nc, esn, "a_tmpo%d" % c, [128, 4, 128], F32),
                        TMPU=sb(nc, esn, "a_tmpu%d" % c, [128, 4, 32], F32)))
                P.ins('dve', lambda eng: eng.memset(VSk[:, :, 128:129], 1.0), writes=['VSk1'])
                P.ins('dve', lambda eng: eng.memset(VWk[:, :, 128:129], 1.0), writes=['VWk1'])
                cpj = [0]

                def interleave(gens):
                    gens = list(gens)
                    while gens:
                        for g_ in list(gens):
                            try:
                                next(g_)
                            except StopIteration:
                                gens.remove(g_)

                def chain_gen(tasks):
                    for t_ in tasks:
                        yield from t_

                def finish_tt(ch, g, tt, gate_col, first, with_imp):
                    c = ch['c']
                    W = 161 if with_imp else 129
                    orw, korw = ch['ORW'].next()
                    RZ, SC, TMPO, TMPU = ch['RZ'], ch['SC'], ch['TMPO'], ch['TMPU']
                    for half in range(2):
                        bo = ch['ob'][half]
                        P.copy('dve', orw[:, 2 * half:2 * half + 2, 0:W],
                               PS[bo][:, 0:2 * W].rearrange("p (a b) -> p a b", a=2), reads=[('ps', bo)], writes=[korw])
                    P.ts('dve', RZ[:], orw[:, :, 128:129], 1e-30, None, ALU.max, None, reads=[korw], writes=[('RZ', c)])
                    P.ins('dve', lambda eng: eng.reciprocal(RZ[:], RZ[:]), reads=[('RZ', c)], writes=[('RZ', c)])
                    i0 = 4 * tt
                    if with_imp:
                        P.tt('dve', TMPU[:], orw[:, :, 129:161], RZ[:].to_broadcast([128, 4, 32]), ALU.mult,
                             reads=[korw, ('RZ', c)], writes=[('TMPU', c)])
                        P.tt('dve', IMP[:, i0:i0 + 4, :], IMP[:, i0:i0 + 4, :], TMPU[:], ALU.add,
                             reads=[('IMP', tt), ('TMPU', c)], writes=[('IMP', tt)])
                    P.tt('dve', SC[:], RZ[:], GTM[:, i0:i0 + 4, gate_col:gate_col + 1], ALU.mult,
                         reads=[('RZ', c), 'GTM'], writes=[('SC', c)])
                    if first:
                        P.tt('pool', OA[g][:, i0:i0 + 4, :], orw[:, :, 0:128], SC[:].to_broadcast([128, 4, 128]), ALU.mult,
                             reads=[korw, ('SC', c)], writes=[('OA', g, tt)])
                    else:
                        P.tt('dve', TMPO[:], orw[:, :, 0:128], SC[:].to_broadcast([128, 4, 128]), ALU.mult,
                             reads=[korw, ('SC', c)], writes=[('TMPO', c)])
                        P.tt('pool', OA[g][:, i0:i0 + 4, :], OA[g][:, i0:i0 + 4, :], TMPO[:], ALU.add,
                             reads=[('OA', g, tt), ('TMPO', c)], writes=[('OA', g, tt)])

                def s_bank(ch):
                    b = ch['sbk'][ch['si'][0] % 2]
                    ch['si'][0] += 1
                    return b

                def cmp_task(ch, k, g):
                    h = 4 * k + g
                    for tt in range(NT):
                        tsl = slice(tt * 512, (tt + 1) * 512)
                        b = s_bank(ch)
                        P.mm(PS[b][0:NCMP, :], KCT[:, k, 0:NCMP], QAg[g][:, tsl], True, False,
                             reads=[('KCT', k), ('QAg', g)], writes=[('ps', b)], inc=False)
                        P.mm(PS[b][0:NCMP, :], ident[0:NCMP, 0:NCMP], negvalid[0:NCMP, tsl], False, True,
                             reads=[('const', 'ident'), ('const', 'negvalid')], writes=[('ps', b)], inc=True)
                        et, ket = ch['ET'].next()
                        P.act(et[0:NCMP, :], PS[b][0:NCMP, :], AF.Exp, reads=[('ps', b)], writes=[ket], scale=SCALE)
                        for cc in range(4):
                            bo = ch['ob'][cc // 2]
                            o0 = (cc % 2) * 161
                            P.mm(PS[bo][:, o0:o0 + 161], et[0:NCMP, cc * 128:(cc + 1) * 128], VCX[0:NCMP, k, :],
                                 True, True, reads=[ket, ('VCX', k), ('VCXc', k)], writes=[('ps', bo)], inc=(cc % 2 == 1))
                        finish_tt(ch, g, tt, 0 * 16 + h, True, True)
                        yield

                def att_task(ch, k, g, brn):
                    h = 4 * k + g
                    if brn == 'slc':
                        Kk, kK, Vk, kV, gbase = KSk, 'KSk', VSk, ('VSk', 'VSk1'), 16
                    else:
                        Kk, kK, Vk, kV, gbase = KWk, 'KWk', VWk, ('VWk', 'VWk1'), 32
                    for tt in range(NT):
                        tsl = slice(tt * 512, (tt + 1) * 512)
                        a_lo = 0 if brn == 'slc' else max(0, 4 * tt - 4)
                        a_hi = 4 * tt + 3
                        first_a = {}
                        last_a = {}
                        started = {}
                        for cc in range(4):
                            first_a[cc] = 0 if brn == 'slc' else max(a_lo, 4 * tt + cc - 4)
                            last_a[cc] = 4 * tt + cc

                        def pv(a, et, ket):
                            cs_ = [cc for cc in range(4) if first_a[cc] <= a <= last_a[cc]]
                            for cc in cs_:
                                bo = ch['ob'][cc // 2]
                                o0 = (cc % 2) * 129
                                P.mm(PS[bo][:, o0:o0 + 129], et[:, cc * 128:(cc + 1) * 128], Vk[:, a, :],
                                     bo not in started, a == last_a[cc], reads=[ket, kV[0], kV[1]],
                                     writes=[('ps', bo)], inc=(cc == cs_[-1]))
                                started[bo] = True
                        prev = None
                        for a in range(a_lo, a_hi + 1):
                            r = a - 4 * tt
                            b = s_bank(ch)
                            need_w = (brn == 'win') or (r >= 0)
                            has_sel = (brn == 'slc' and tt >= 2)
                            only_qk = not (has_sel or need_w)
                            P.mm(PS[b][:], Kk[:, a * 128:(a + 1) * 128], QAg[g][:, tsl], True, only_qk,
                                 reads=[kK, ('QAg', g)], writes=[('ps', b)], inc=only_qk)
                            if brn == 'slc' and tt >= 2:
                                P.mm(PS[b][:], expm[0:32, a, :], NEGSEL[:, tsl], False, not need_w,
                                     reads=[('const', 'expm'), 'NEGSEL'], writes=[('ps', b)], inc=not need_w)
                            if need_w:
                                P.mm(PS[b][:], ident[:], negw[:, r + 4, :], False, True,
                                     reads=[('const', 'ident'), ('const', 'negw')], writes=[('ps', b)], inc=True)
                            et, ket = ch['ET'].next()
                            P.act(et[:], PS[b][:], AF.Exp, reads=[('ps', b)], writes=[ket], scale=SCALE)
                            if prev is not None:
                                pv(*prev)
                            prev = (a, et, ket)
                            yield
                        pv(*prev)
                        finish_tt(ch, g, tt, gbase + h, False, False)
                        yield

                for k in range(4):
                    for g in range(4):
                        P.dma('sp', QAg[g][:], QA[4 * k + g], writes=[('QAg', g)])
                    P.dma('sp', KSk[:], KS[k], writes=['KSk'])
                    P.dma('sp', KWk[:], KW[k], writes=['KWk'])
                    P.dma('sp', VSk[:, :, 0:128], VS[:, k * 128:(k + 1) * 128].rearrange("(a p) d -> p a d", p=128),
                          writes=['VSk'])
                    P.dma('sp', VWk[:, :, 0:128], VW[:, k * 128:(k + 1) * 128].rearrange("(a p) d -> p a d", p=128),
                          writes=['VWk'])
                    P.ins('dve', lambda eng: eng.memset(IMP[:], 0.0), writes=[('IMP', tt) for tt in range(NT)])
                    interleave([chain_gen([cmp_task(CH[0], k, 0), cmp_task(CH[0], k, 1)]),
                                chain_gen([cmp_task(CH[1], k, 2), cmp_task(CH[1], k, 3)])])
                    P.tt('dve', IMF[:], IMP[:], vnf[:], ALU.mult, reads=[('IMP', tt) for tt in range(NT)] + [('const', 'vnf')],
                         writes=['IMF'])
                    P.tt('dve', IMF[:], IMF[:], addc[:], ALU.add, reads=['IMF', ('const', 'addc')], writes=['IMF'])
                    for i in range(NI):
                        P.ins('dve', lambda eng, i=i: eng.max(M8[:, i, :], IMF[:, i, :]), reads=['IMF'],
                              writes=[('M8', i)])
                        P.ins('dve', lambda eng, i=i: eng.match_replace(IM2[:, i, :], M8[:, i, :], IMF[:, i, :], -3.0e38),
                              reads=['IMF', ('M8', i)], writes=[('IM2', i)])
                        P.ins('dve', lambda eng, i=i: eng.max(M8b[:, i, :], IM2[:, i, :]), reads=[('IM2', i)],
                              writes=[('M8b', i)])
                        P.ts('dve', NSL[:, i, :], IMF[:, i, :], M8b[:, i, 7:8], -BIG, ALU.is_lt, ALU.mult,
                             reads=['IMF', ('M8b', i)], writes=[('NSL', i)])
                    for half in range(2):
                        b = 0
                        pb = PS[b][:].bitcast(BF16)
                        for m in range(8):
                            i = half * 8 + m
                            P.transpose(pb[0:32, m * 128:(m + 1) * 128], NSL[:, i, :], ident[:],
                                        reads=[('NSL', i), ('const', 'ident')], writes=[('ps', b)], inc=(m == 7))
                        P.copy('dve', NEGSEL[:, half * 1024:(half + 1) * 1024], pb[0:32, :], reads=[('ps', b)],
                               writes=['NEGSEL'])
                    interleave([chain_gen([att_task(CH[0], k, 0, 'slc'), att_task(CH[0], k, 1, 'slc'),
                                           att_task(CH[0], k, 0, 'win'), att_task(CH[0], k, 1, 'win')]),
                                chain_gen([att_task(CH[1], k, 2, 'slc'), att_task(CH[1], k, 3, 'slc'),
                                           att_task(CH[1], k, 2, 'win'), att_task(CH[1], k, 3, 'win')])])
                    for g in range(4):
                        h = 4 * k + g
                        for i4 in range(NI // 4):
                            b = cpj[0] % 2
                            for m in range(4):
                                i = i4 * 4 + m
                                P.transpose(PS[b][:, m * 128:(m + 1) * 128], OA[g][:, i, :], identf[:],
                                            reads=[('OA', g, i4), ('const', 'ident')], writes=[('ps', b)], inc=(m == 3))
                            P.copy('act' if cpj[0] % 2 == 0 else 'dve', OTA[:, h, i4 * 512:(i4 + 1) * 512], PS[b][:],
                                   reads=[('ps', b)], writes=[('OT', h)])
                            cpj[0] += 1
                P.barrier()
                P.emit()
        if stop <= 3:
            return
        OTB = sb(nc, esO, "a_OTB", [128, 16, T], BF16)
        if True:
            with ExitStack() as ess:
                ident = CB.load(P, nc, ess, blob_d, 'ident', BF16)
                tri = CB.load(P, nc, ess, blob_d, 'tri', BF16)
                omt = CB.load(P, nc, ess, blob_d, 'onesmtri', BF16)
                negsb = CB.load(P, nc, ess, blob_d, 'negsb', BF16)
                QBh = Rot('QBh', [sb(nc, ess, "a_qb%d" % i, [128, T], BF16) for i in range(2)])
                KBh = Rot('KBh', [sb(nc, ess, "a_kb%d" % i, [128, T], BF16) for i in range(2)])
                VBh = Rot('VBh', [sb(nc, ess, "a_vb%d" % i, [128, NI, 128], BF16) for i in range(2)])
                CHS = []
                for c in range(2):
                    CHS.append(dict(
                        c=c, sbk=[4 * c, 4 * c + 1], X=4 * c + 2, O=4 * c + 3, si=[0],
                        SP=Rot('sbsp%d' % c, [sb(nc, ess, "a_ssp%d_%d" % (c, i), [128, 512], F32) for i in range(2)]),
                        SPB=Rot('sbspb%d' % c, [sb(nc, ess, "a_sspb%d_%d" % (c, i), [128, 512], BF16) for i in range(4)]),
                        U=Rot('sbu%d' % c, [sb(nc, ess, "a_su%d_%d" % (c, i), [128, 512], F32) for i in range(4)]),
                        A=Rot('sba%d' % c, [sb(nc, ess, "a_sa%d_%d" % (c, i), [128, 512], BF16) for i in range(3)])))
                LA = 3

                def interleave2(gens):
                    gens = list(gens)
                    while gens:
                        for g_ in list(gens):
                            try:
                                next(g_)
                            except StopIteration:
                                gens.remove(g_)

                def load_head(h):
                    q, kq = QBh.next()
                    kk, kkk = KBh.next()
                    v, kv = VBh.next()
                    P.dma('sp', q[:], QB[h], writes=[kq])
                    P.dma('sp', kk[:], KB[h], writes=[kkk])
                    P.dma('sp', v[:], VB[:, h * 128:(h + 1) * 128].rearrange("(a p) d -> p a d", p=128), writes=[kv])
                    return (q, kq, kk, kkk, v, kv)

                def sb_sweep(ch, head, h, tt):
                    (q, kq, kk, kkk, v, kv) = head
                    tsl = slice(tt * 512, (tt + 1) * 512)
                    amax = 4 * tt + 3
                    bX = ch['X']
                    bO = ch['O']
                    st = {}

                    def stage1(a):
                        r = a - 4 * tt
                        b = ch['sbk'][ch['si'][0] % 2]
                        ch['si'][0] += 1
                        P.mm(PS[b][:], kk[:, a * 128:(a + 1) * 128], q[:, tsl], True, r < 0,
                             reads=[kkk, kq], writes=[('ps', b)], inc=(r < 0))
                        if r >= 0:
                            P.mm(PS[b][:], ident[:], negsb[:, r, :], False, True,
                                 reads=[('const', 'ident'), ('const', 'negsb')], writes=[('ps', b)], inc=True)
                        sp, ksp = ch['SP'].next()
                        spb, kspb = ch['SPB'].next()
                        u, ku = ch['U'].next()
                        P.act(sp[:], PS[b][:], AF.Exp, reads=[('ps', b)], writes=[ksp], scale=SCALE)
                        P.act(sp[:], sp[:], AF.Ln, reads=[ksp], writes=[ksp], bias=1.0)
                        P.copy('pool' if ch['c'] == 1 else 'dve', spb[:], sp[:], reads=[ksp], writes=[kspb])
                        P.stt('dve', u[:], PS[b][:], SCALE, sp[:], ALU.mult, ALU.subtract,
                              reads=[('ps', b), ksp], writes=[ku])
                        st[a] = (spb, kspb, u, ku)

                    def pv(a, A, kA):
                        P.mm(PS[bO][:], v[:, a, :], A[:], a == amax, a == 0, reads=[kv, kA],
                             writes=[('ps', bO)], inc=True)
                    nxt1 = amax
                    for _ in range(LA):
                        if nxt1 >= 0:
                            stage1(nxt1)
                            nxt1 -= 1
                    yield
                    prevA = None
                    for a in range(amax, -1, -1):
                        spb, kspb, u, ku = st.pop(a)
                        P.mm(PS[bX][:], tri[:], spb[:], a == amax, a == 0, reads=[('const', 'tri'), kspb],
                             writes=[('ps', bX)], inc=True)
                        P.tt('dve', u[:], u[:], PS[bX][:], ALU.subtract, reads=[ku, ('ps', bX)], writes=[ku])
                        yield
                        if nxt1 >= 0:
                            stage1(nxt1)
                            nxt1 -= 1
                        yield
                        if a > 0:
                            P.mm(PS[bX][:], omt[:], spb[:], False, False, reads=[('const', 'onesmtri'), kspb],
                                 writes=[('ps', bX)], inc=True)
                        A, kA = ch['A'].next()
                        P.act(A[:], u[:], AF.Exp, reads=[ku], writes=[kA])
                        if prevA is not None:
                            pv(*prevA)
                        prevA = (a, A, kA)
                        yield
                    pv(*prevA)
                    P.copy('dve', OTB[:, h, tsl], PS[bO][:], reads=[('ps', bO)], writes=[('OT', 16 + h)])
                    yield

                def sb_chain(ch, head, h, tts):
                    for tt in tts:
                        yield from sb_sweep(ch, head, h, tt)
                nxt = load_head(0)
                for h in range(16):
                    head = nxt
                    if h + 1 < 16:
                        nxt = load_head(h + 1)
                    interleave2([sb_chain(CHS[0], head, h, [3, 0]), sb_chain(CHS[1], head, h, [2, 1])])
                P.barrier()
                P.emit()
        if stop <= 4:
            return
        with ExitStack() as es:
            wov = w_out.rearrange("(c p) n -> p c n", p=128)
            wo = Rot('wo', [sb(nc, es, "a_wo%d" % i, [128, 32, 512], BF16) for i in range(2)])
            xr = Rot('xr', [sb(nc, es, "a_xr%d" % i, [128, 512], F32) for i in range(2)])
            yo = Rot('yo', [sb(nc, es, "a_yo%d" % i, [128, 512], F32) for i in range(2)])
            if STd is not None:
                ST = sb(nc, es, "a_ST", [128, NI, D // 512, 6], F32)
            for db in range(D // 512):
                w, kw_ = wo.next()
                P.dma('pool', w[:], wov[:, :, db * 512:(db + 1) * 512], writes=[kw_])
                for i in range(NI):
                    x_, kx = xr.next()
                    y_, ky = yo.next()
                    P.dma('act', x_[:], x_in[i * 128:(i + 1) * 128, db * 512:(db + 1) * 512], writes=[kx])
                    b = P.next_bank()
                    for c in range(32):
                        P.mm(PS[b][:], (OTA if c < 16 else OTB)[:, c % 16, i * 128:(i + 1) * 128], w[:, c, :], c == 0, c == 31,
                             reads=[kw_], writes=[('ps', b)], inc=(c == 31))
                    P.stt('dve', y_[:], x_[:], ALPHA, PS[b][:], ALU.mult, ALU.add, reads=[kx, ('ps', b)], writes=[ky])
                    if STd is not None:
                        P.ins('dve', lambda eng, o=ST[:, i, db, :], a=y_[:]: eng.bn_stats(o, a),
                              reads=[ky], writes=[('ST', i, db)])
                    P.dma('sp', Y[i * 128:(i + 1) * 128, db * 512:(db + 1) * 512], y_[:], reads=[ky], writes=['Y'])
            if STd is not None:
                P.dma('sp', STd, ST[:].rearrange("p a b c -> p (a b c)"),
                      reads=[('ST', i, db) for i in range(NI) for db in range(D // 512)], writes=['STd'])
            P.barrier()
            P.emit()
    o_i, n_i, _ = CB.off['ident']
    return final_ln_phase(P, nc, PS, blob_d[:, o_i:o_i + n_i], Y, x_out, lng, lnb, T, D, "al", want_xt, STd=STd)


def attn_scratch(nc, T, D, kind="Internal"):
    S = {}
    S['QA'] = nc.dram_tensor("s_QA", [16, 128, T], BF16, kind=kind).ap()
    S['KCR'] = nc.dram_tensor("s_KCR", [4, 128, T], BF16, kind=kind).ap()
    S['VCR'] = nc.dram_tensor("s_VCR", [4, 128, T], BF16, kind=kind).ap()
    S['KS'] = nc.dram_tensor("s_KS", [4, 128, T], BF16, kind=kind).ap()
    S['KW'] = nc.dram_tensor("s_KW", [4, 128, T], BF16, kind=kind).ap()
    S['QB'] = nc.dram_tensor("s_QB", [16, 128, T], BF16, kind=kind).ap()
    S['KB'] = nc.dram_tensor("s_KB", [16, 128, T], BF16, kind=kind).ap()
    S['VS'] = nc.dram_tensor("s_VS", [T, 512], BF16, kind=kind).ap()
    S['VW'] = nc.dram_tensor("s_VW", [T, 512], BF16, kind=kind).ap()
    S['VB'] = nc.dram_tensor("s_VB", [T, 2048], BF16, kind=kind).ap()
    S['GTM'] = nc.dram_tensor("s_GTM", [128, (T // 128) * 48], F32, kind=kind).ap()
    S['Y'] = nc.dram_tensor("s_Ya", [T, D], F32, kind=kind).ap()
    return S


T_SEQ = 2048
D_MODEL = 4096
FC_FF = 86
FUSED = True
_CB = None
_PROG_CACHE = {}


def get_cb():
    global _CB
    if _CB is None:
        c = attn_consts()
        c['poolc'] = pool_consts()
        _CB = ConstBlob(c)
    return _CB


def build_program(stages):
    T, D, FC = T_SEQ, D_MODEL, FC_FF
    F = FC * 128
    CB = get_cb()
    nc = bass.Bass("TRN2", target_bir_lowering=False)

    def ein(name, shape):
        return nc.dram_tensor(name, list(shape), F32, kind="ExternalInput").ap()
    x = ein("x", [T, D])
    blob = ein("blob", [128, CB.ncols])
    y = nc.dram_tensor("y", [T, D], F32, kind="ExternalOutput").ap()
    io = {}
    if 'attn' in stages:
        io['attn'] = dict(w_in=ein("w_in", [D, IN_DIM]), w_out=ein("w_out", [4096, D]), cwk=ein("cwk", [32, 128, 128]),
                          cwv=ein("cwv", [32, 128, 128]), pekT=ein("pekT", [128, 32]), pevT=ein("pevT", [128, 32]),
                          lng=ein("lng_a", [1, D]), lnb=ein("lnb_a", [1, D]))
    for l in (0, 1):
        if 'ffn%d' % l in stages:
            io['ffn%d' % l] = dict(w_up=ein("w_up%d" % l, [D, 2 * F]), convp=ein("convp%d" % l, [128, 8 * FC]),
                                   w_down=ein("w_down%d" % l, [F, D]), lng=ein("lng_f%d" % l, [1, D]),
                                   lnb=ein("lnb_f%d" % l, [1, D]))
    if 'pool' in stages:
        io['pool'] = dict(pool_w=ein("pool_w", [4, D // 4, D // 4]), pool_scale=ein("pool_scale", [1, D]),
                          lng=ein("lng_p", [1, D]), lnb=ein("lnb_p", [1, D]))
    Yd = nc.dram_tensor("s_Y", [T, D], F32).ap()
    STd = nc.dram_tensor("s_ST", [128, (T // 128) * (D // 512) * 6], F32).ap()
    G = None
    if 'ffn0' in stages or 'ffn1' in stages:
        G = nc.dram_tensor("s_G", [T // 128, 128, FC, 128], BF16).ap()
    S = None
    if 'attn' in stages:
        S = attn_scratch(nc, T, D)
    cur = x
    o_id, n_id, _ = CB.off['ident']
    o_pc, n_pc, _ = CB.off['poolc']
    with ExitStack() as es:
        P = Prog(nc, es)
        PS = [es.enter_context(nc.psum_tensor("ps%d" % i, [128, 512], F32)) for i in range(8)]
        xt = None
        for si, st in enumerate(stages):
            dst = y if si == len(stages) - 1 else nc.dram_tensor("s_X%d" % si, [T, D], F32).ap()
            a = io[st]
            nxt_ffn = si + 1 < len(stages) and stages[si + 1].startswith('ffn')
            if st == 'attn':
                xt = stage_attn(P, nc, PS, CB, blob, cur, dst, a['w_in'], a['w_out'], a['cwk'], a['cwv'], a['pekT'],
                                a['pevT'], a['lng'], a['lnb'], S, T, D, want_xt=nxt_ffn, STd=STd)
            elif st == 'pool':
                xt = stage_pool(P, nc, PS, cur, dst, a['pool_w'], a['pool_scale'],
                                blob[:, o_pc:o_pc + n_pc].rearrange("p (a b) -> p a b", a=12), a['lng'], a['lnb'], Yd, T, D,
                                ident_d=blob[:, o_id:o_id + n_id], want_xt=nxt_ffn, STd=STd)
            else:
                stage_ffn(P, nc, PS, blob[:, o_id:o_id + n_id], cur, dst, a['w_up'], a['convp'], a['w_down'],
                          a['lng'], a['lnb'], G, Yd, T, D, FC, xt_in=xt, STd=STd)
                xt = None
            cur = dst
    return nc


def _prog(stages):
    key = tuple(stages)
    if key not in _PROG_CACHE:
        _PROG_CACHE[key] = build_program(list(stages))
    return _PROG_CACHE[key]


def _f32(a):
    return np.ascontiguousarray(np.asarray(a, dtype=np.float32))


def kernel(x, attn_w_in, attn_w_out, cmp_w_k, cmp_w_v, cmp_pe_k, cmp_pe_v, pool_w, pool_scale,
           ffn_w_up, ffn_conv_w, ffn_conv_b, ffn_w_down, ln_mix_g, ln_mix_b, ln_ffn_g, ln_ffn_b):
    n = 8
    CB = get_cb()
    x = _f32(x)
    FC = FC_FF
    shared = {}
    shared['attn'] = dict(w_in=_f32(attn_w_in)[0], w_out=_f32(attn_w_out)[0], cwk=_f32(cmp_w_k)[0], cwv=_f32(cmp_w_v)[0],
                          pekT=np.ascontiguousarray(_f32(cmp_pe_k)[0].T), pevT=np.ascontiguousarray(_f32(cmp_pe_v)[0].T),
                          lng_a=_f32(ln_mix_g)[0:1], lnb_a=_f32(ln_mix_b)[0:1])
    for l in (0, 1):
        cw4 = np.concatenate([_f32(ffn_conv_w)[l], _f32(ffn_conv_b)[l][None]], 0)
        convp = np.ascontiguousarray(cw4.reshape(4, 2 * FC, 128).transpose(2, 0, 1)).reshape(128, 8 * FC)
        shared['ffn%d' % l] = {"w_up%d" % l: _f32(ffn_w_up)[l], "convp%d" % l: convp, "w_down%d" % l: _f32(ffn_w_down)[l],
                               "lng_f%d" % l: _f32(ln_ffn_g)[l:l + 1], "lnb_f%d" % l: _f32(ln_ffn_b)[l:l + 1]}
    shared['pool'] = dict(pool_w=_f32(pool_w)[0], pool_scale=_f32(pool_scale)[0:1],
                          lng_p=_f32(ln_mix_g)[1:2], lnb_p=_f32(ln_mix_b)[1:2])
    order = ['attn', 'ffn0', 'pool', 'ffn1']
    groups = [order] if FUSED else [[s] for s in order]
    cur = [x[c] for c in range(n)]
    for stages in groups:
        nc = _prog(stages)
        base = {"blob": CB.arr}
        for s in stages:
            base.update(shared[s])
        in_maps = []
        for c in range(n):
            m = dict(base)
            m["x"] = cur[c]
            in_maps.append(m)
        res = run_bass_kernel_spmd(nc, in_maps, core_ids=list(range(n)))
        cur = [np.asarray(res.results[c]["y"]) for c in range(n)]
    return np.stack(cur, 0).astype(np.float32)
```
